# Optimizing a Trainium2 kernel written in Bass

```python
import math
import jax, jax.numpy as jnp
from jax import lax
import numpy as np

D_MODEL = 1024
BATCH = 2
SEQ = 8192
DEPTH = 4

N_MIXERS = 3
N_S5_LAYERS = (DEPTH + 2) // 3
N_HG_LAYERS = (DEPTH + 1) // 3
N_AT_LAYERS = DEPTH // 3

S5_WIDTH = D_MODEL
S5_GROUP = 16
S5_GROUPS = S5_WIDTH // S5_GROUP
S5_STATE = 64
DT_MIN = 1e-3
DT_MAX = 1e-1

HG_WIDTH = D_MODEL
HG_HEAD_DIM = 128
HG_HEADS = HG_WIDTH // HG_HEAD_DIM
HG_CHUNK = 64

AT_HEAD_DIM = 64
AT_Q_HEADS = D_MODEL // AT_HEAD_DIM
AT_KV_HEADS = 4
AT_GROUP = AT_Q_HEADS // AT_KV_HEADS
AT_Q_WIDTH = AT_Q_HEADS * AT_HEAD_DIM
AT_KV_WIDTH = AT_KV_HEADS * AT_HEAD_DIM
AT_QKV_WIDTH = AT_Q_WIDTH + 2 * AT_KV_WIDTH
WINDOW = 128
BLOCK = 128
ROPE_THETA = 10000.0

NORM_EPS = 1e-6

kernel_name = 'hybrid_s5_hgrn2_swa_trunk'


def rms_norm(x, g):
    xf = x.astype(jnp.float32)
    y = xf * lax.rsqrt(jnp.mean(xf * xf, axis=-1, keepdims=True) + NORM_EPS)
    return (y * g.astype(jnp.float32)).astype(x.dtype)


def _complex_scan_combine(e1, e2):
    a1r, a1i, b1r, b1i = e1
    a2r, a2i, b2r, b2i = e2
    return (a2r * a1r - a2i * a1i,
            a2r * a1i + a2i * a1r,
            a2r * b1r - a2i * b1i + b2r,
            a2r * b1i + a2i * b1r + b2i)


def s5_branch(h, w_in, lam_re, lam_im, log_dt, b_re, b_im, c_re, c_im, d_skip, w_glu, b_glu, w_out):
    bsz, seq, _ = h.shape
    proj = h @ w_in
    u, z = proj[..., :S5_WIDTH], proj[..., S5_WIDTH:]
    uf = u.astype(jnp.float32).reshape(bsz, seq, S5_GROUPS, S5_GROUP)
    lr = lam_re.astype(jnp.float32)
    li = lam_im.astype(jnp.float32)
    dt = jnp.exp(log_dt.astype(jnp.float32))[:, None]
    mag = jnp.exp(lr * dt)
    ar = mag * jnp.cos(li * dt)
    ai = mag * jnp.sin(li * dt)
    den = lr * lr + li * li
    qr = ((ar - 1.0) * lr + ai * li) / den
    qi = (ai * lr - (ar - 1.0) * li) / den
    br = b_re.astype(jnp.float32)
    bi = b_im.astype(jnp.float32)
    bbr = qr[..., None] * br - qi[..., None] * bi
    bbi = qr[..., None] * bi + qi[..., None] * br
    xr = jnp.einsum('blgh,gph->blgp', uf, bbr)
    xi = jnp.einsum('blgh,gph->blgp', uf, bbi)
    a_r = jnp.broadcast_to(ar, (1, seq) + ar.shape)
    a_i = jnp.broadcast_to(ai, (1, seq) + ai.shape)
    _, _, sr, si = lax.associative_scan(_complex_scan_combine, (a_r, a_i, xr, xi), axis=1)
    y = (jnp.einsum('ghp,blgp->blgh', c_re.astype(jnp.float32), sr)
         - jnp.einsum('ghp,blgp->blgh', c_im.astype(jnp.float32), si))
    y = y.reshape(bsz, seq, S5_WIDTH) + d_skip.astype(jnp.float32) * u.astype(jnp.float32)
    y = jax.nn.gelu(y).astype(h.dtype)
    y = y * jax.nn.sigmoid(y @ w_glu + b_glu)
    return (y * jax.nn.silu(z)) @ w_out


def hgrn2_lower_bounds(lb_logits):
    p = jax.nn.softmax(lb_logits.astype(jnp.float32), axis=0)
    return jnp.cumsum(p, axis=0) - p[0]


def _to_chunks(t):
    bsz, seq, nh, dh = t.shape
    return t.reshape(bsz, seq // HG_CHUNK, HG_CHUNK, nh, dh).transpose(1, 0, 3, 2, 4)


def _hgrn2_chunk_step(state, inp):
    q, k, v, logf = inp
    b = jnp.cumsum(logf, axis=2)
    o_inter = jnp.einsum('bhtk,bhkv->bhtv', q * jnp.exp(b), state)
    causal = jnp.tril(jnp.ones((HG_CHUNK, HG_CHUNK), dtype=bool))
    diff = b[:, :, :, None, :] - b[:, :, None, :, :]
    decay = jnp.exp(jnp.where(causal[:, :, None], diff, -jnp.inf))
    scores = jnp.einsum('bhtk,bhtsk,bhsk->bhts', q, decay, k)
    o_intra = jnp.einsum('bhts,bhsv->bhtv', scores, v)
    b_last = b[:, :, -1:, :]
    new_state = (jnp.exp(b_last[:, :, 0, :])[..., None] * state
                 + jnp.einsum('bhsk,bhsv->bhkv', k * jnp.exp(b_last - b), v))
    return new_state, o_inter + o_intra


def hgrn2_branch(h, w_in, lb, norm_g, w_out):
    bsz, seq, _ = h.shape
    proj = h @ w_in
    q, fz, i_in, z = jnp.split(proj, 4, axis=-1)
    f = lb + (1.0 - lb) * jax.nn.sigmoid(fz.astype(jnp.float32))
    logf = jnp.log(f)
    k = 1.0 - f
    heads = lambda t: _to_chunks(t.astype(jnp.float32).reshape(bsz, seq, HG_HEADS, HG_HEAD_DIM))
    s0 = jnp.zeros((bsz, HG_HEADS, HG_HEAD_DIM, HG_HEAD_DIM), jnp.float32)
    _, o = lax.scan(_hgrn2_chunk_step, s0, (heads(q), heads(k), heads(i_in), heads(logf)))
    o = o.transpose(1, 0, 3, 2, 4).reshape(bsz, seq, HG_HEADS, HG_HEAD_DIM)
    o = rms_norm(o, norm_g.reshape(HG_HEADS, HG_HEAD_DIM)).reshape(bsz, seq, HG_WIDTH).astype(h.dtype)
    return (o * jax.nn.silu(z)) @ w_out


def _rope(t, cos, sin):
    half = AT_HEAD_DIM // 2
    t1 = t[..., :half].astype(jnp.float32)
    t2 = t[..., half:].astype(jnp.float32)
    return jnp.concatenate([t1 * cos - t2 * sin, t2 * cos + t1 * sin], axis=-1).astype(t.dtype)


def swa_branch(h, positions, w_in, b_in, sinks, w_out):
    bsz, seq, _ = h.shape
    nb = seq // BLOCK
    proj = h @ w_in
    qkv = proj[..., :AT_QKV_WIDTH] + b_in
    z = proj[..., AT_QKV_WIDTH:]
    q = qkv[..., :AT_Q_WIDTH].reshape(bsz, seq, AT_Q_HEADS, AT_HEAD_DIM)
    k = qkv[..., AT_Q_WIDTH:AT_Q_WIDTH + AT_KV_WIDTH].reshape(bsz, seq, AT_KV_HEADS, AT_HEAD_DIM)
    v = qkv[..., AT_Q_WIDTH + AT_KV_WIDTH:].reshape(bsz, seq, AT_KV_HEADS, AT_HEAD_DIM)
    inv_freq = ROPE_THETA ** (-jnp.arange(0, AT_HEAD_DIM, 2, dtype=jnp.float32) / AT_HEAD_DIM)
    ang = positions.astype(jnp.float32)[..., None] * inv_freq
    cos = jnp.cos(ang)[:, :, None, :]
    sin = jnp.sin(ang)[:, :, None, :]
    q = _rope(q, cos, sin)
    k = _rope(k, cos, sin)
    qb = q.reshape(bsz, nb, BLOCK, AT_KV_HEADS, AT_GROUP, AT_HEAD_DIM)

    def band(t):
        t = t.reshape(bsz, nb, BLOCK, AT_KV_HEADS, AT_HEAD_DIM)
        prev = jnp.pad(t[:, :-1], ((0, 0), (1, 0), (0, 0), (0, 0), (0, 0)))
        return jnp.concatenate([prev, t], axis=2)

    kb, vb = band(k), band(v)
    s = jnp.einsum('bnqhgd,bnkhd->bnhgqk', qb, kb).astype(jnp.float32) * (AT_HEAD_DIM ** -0.5)
    qi = jnp.arange(BLOCK)[:, None]
    kj = jnp.arange(2 * BLOCK)[None, :]
    dist = qi + BLOCK - kj
    in_window = (dist >= 0) & (dist < WINDOW)
    has_prev = (jnp.arange(nb)[:, None, None] > 0) | (kj >= BLOCK)[None]
    mask = in_window[None] & has_prev
    s = jnp.where(mask[None, :, None, None], s, -jnp.inf)
    sink = sinks.astype(jnp.float32).reshape(AT_KV_HEADS, AT_GROUP)[None, None, :, :, None, None]
    m = jnp.maximum(jnp.max(s, axis=-1, keepdims=True), sink)
    e = jnp.exp(s - m)
    p = e / (jnp.sum(e, axis=-1, keepdims=True) + jnp.exp(sink - m))
    o = jnp.einsum('bnhgqk,bnkhd->bnqhgd', p.astype(vb.dtype), vb).reshape(bsz, seq, AT_Q_WIDTH)
    return (o * jax.nn.silu(z)) @ w_out


def setup_inputs(seed: int = 0) -> dict:
    key = jax.random.key(seed)
    ks = iter(jax.random.split(key, 32))

    def nrm(shape, scale):
        return jax.random.normal(next(ks), shape, jnp.float32) * scale

    x = nrm((BATCH, SEQ, D_MODEL), 1.0)
    offset = jax.random.randint(next(ks), (BATCH, 1), 0, 4096, dtype=jnp.int32)
    positions = (offset + jnp.arange(SEQ, dtype=jnp.int32)[None, :]).astype(jnp.int32)
    norm_pre = 1.0 + nrm((DEPTH, D_MODEL), 0.02)
    norm_post = 1.0 + nrm((DEPTH, D_MODEL), 0.02)

    nA = N_S5_LAYERS
    s5_w_in = nrm((nA, D_MODEL, 2 * S5_WIDTH), D_MODEL ** -0.5)
    s5_lambda_re = -0.5 + nrm((nA, S5_GROUPS, S5_STATE), 0.01)
    s5_lambda_im = (math.pi * jnp.arange(S5_STATE, dtype=jnp.float32))[None, None, :] + nrm((nA, S5_GROUPS, S5_STATE), 0.01)
    s5_log_dt = jax.random.uniform(next(ks), (nA, S5_GROUPS), jnp.float32, math.log(DT_MIN), math.log(DT_MAX))
    s5_b_re = nrm((nA, S5_GROUPS, S5_STATE, S5_GROUP), (2.0 * S5_GROUP) ** -0.5)
    s5_b_im = nrm((nA, S5_GROUPS, S5_STATE, S5_GROUP), (2.0 * S5_GROUP) ** -0.5)
    s5_c_re = nrm((nA, S5_GROUPS, S5_GROUP, S5_STATE), (2.0 * S5_STATE) ** -0.5)
    s5_c_im = nrm((nA, S5_GROUPS, S5_GROUP, S5_STATE), (2.0 * S5_STATE) ** -0.5)
    s5_d = nrm((nA, S5_WIDTH), 1.0)
    s5_w_glu = nrm((nA, S5_WIDTH, S5_WIDTH), S5_WIDTH ** -0.5)
    s5_b_glu = nrm((nA, S5_WIDTH), 0.02)
    s5_w_out = nrm((nA, S5_WIDTH, D_MODEL), S5_WIDTH ** -0.5)

    nB = N_HG_LAYERS
    hg_w_in = nrm((nB, D_MODEL, 4 * HG_WIDTH), D_MODEL ** -0.5)
    hg_lb_logits = nrm((DEPTH, HG_WIDTH), 0.1)
    hg_norm = 1.0 + nrm((nB, HG_WIDTH), 0.02)
    hg_w_out = nrm((nB, HG_WIDTH, D_MODEL), HG_WIDTH ** -0.5)

    nC = N_AT_LAYERS
    at_w_in = nrm((nC, D_MODEL, AT_QKV_WIDTH + AT_Q_WIDTH), D_MODEL ** -0.5)
    at_b_in = nrm((nC, AT_QKV_WIDTH), 0.02)
    at_sinks = nrm((nC, AT_Q_HEADS), 0.5)
    at_w_out = nrm((nC, AT_Q_WIDTH, D_MODEL), AT_Q_WIDTH ** -0.5)

    return {'x': x, 'positions': positions, 'norm_pre': norm_pre, 'norm_post': norm_post,
            's5_w_in': s5_w_in, 's5_lambda_re': s5_lambda_re, 's5_lambda_im': s5_lambda_im,
            's5_log_dt': s5_log_dt, 's5_b_re': s5_b_re, 's5_b_im': s5_b_im,
            's5_c_re': s5_c_re, 's5_c_im': s5_c_im, 's5_d': s5_d,
            's5_w_glu': s5_w_glu, 's5_b_glu': s5_b_glu, 's5_w_out': s5_w_out,
            'hg_w_in': hg_w_in, 'hg_lb_logits': hg_lb_logits, 'hg_norm': hg_norm, 'hg_w_out': hg_w_out,
            'at_w_in': at_w_in, 'at_b_in': at_b_in, 'at_sinks': at_sinks, 'at_w_out': at_w_out}


def reference(x, positions, norm_pre, norm_post,
              s5_w_in, s5_lambda_re, s5_lambda_im, s5_log_dt, s5_b_re, s5_b_im,
              s5_c_re, s5_c_im, s5_d, s5_w_glu, s5_b_glu, s5_w_out,
              hg_w_in, hg_lb_logits, hg_norm, hg_w_out,
              at_w_in, at_b_in, at_sinks, at_w_out):
    lower_bounds = hgrn2_lower_bounds(hg_lb_logits)
    h = x
    for i in range(DEPTH):
        kind, j = i % N_MIXERS, i // N_MIXERS
        u = rms_norm(h, norm_pre[i])
        if kind == 0:
            y = s5_branch(u, s5_w_in[j], s5_lambda_re[j], s5_lambda_im[j], s5_log_dt[j],
                          s5_b_re[j], s5_b_im[j], s5_c_re[j], s5_c_im[j], s5_d[j],
                          s5_w_glu[j], s5_b_glu[j], s5_w_out[j])
        elif kind == 1:
            y = hgrn2_branch(u, hg_w_in[j], lower_bounds[i], hg_norm[j], hg_w_out[j])
        else:
            y = swa_branch(u, positions, at_w_in[j], at_b_in[j], at_sinks[j], at_w_out[j])
        h = h + rms_norm(y, norm_post[i])
    return h
```

```python
import math
import os
import numpy as np
import ml_dtypes
import concourse.bass as bass
import concourse.mybir as mybir
from concourse.bass_utils import run_bass_kernel_spmd

F32 = mybir.dt.float32
BF16 = mybir.dt.bfloat16
I32 = mybir.dt.int32
AF = mybir.ActivationFunctionType
ALU = mybir.AluOpType
AX = mybir.AxisListType

L = 8192
D = 1024
NBLK = 8
NCORES = 2
EPS = 1e-6
TWO_PI = 2.0 * math.pi


class Sched:
    def __init__(self, nc):
        self.nc = nc
        self.eng = {"pe": nc.tensor, "act": nc.scalar, "dve": nc.vector, "pool": nc.gpsimd, "sp": nc.sync}
        self.esem = {}
        self.ecnt = {}
        self._ctx = []
        self.sems = {}
        for e in self.eng:
            s = nc.semaphore("s_" + e)
            self.esem[e] = s.__enter__()
            self._ctx.append(s)
            self.ecnt[e] = 0
            self.sems["e:" + e] = self.esem[e]
        self.dsem = {}
        self.dcnt = {}
        self.lastw = {}
        self.readers = {}
        self.waited = {}
        self.nins = 0

    def _dslot(self, slot):
        if slot not in self.dsem:
            s = self.nc.semaphore("d_%d" % len(self.dsem))
            self.dsem[slot] = s.__enter__()
            self._ctx.append(s)
            self.dcnt[slot] = 0
            self.sems["d:" + str(slot)] = self.dsem[slot]
        return self.dsem[slot]

    def _wait(self, e, n, v):
        if self.waited.get((e, n), 0) < v:
            self.eng[e].wait_ge(self.sems[n], v)
            self.waited[(e, n)] = v

    def _deps(self, e, reads, writes):
        need = {}

        def add(t):
            if t is None:
                return
            n, v = t
            if need.get(n, 0) < v:
                need[n] = v
        for r in reads:
            add(self.lastw.get(r))
        for w in writes:
            add(self.lastw.get(w))
            for t in self.readers.get(w, ()):
                add(t)
        for n, v in need.items():
            self._wait(e, n, v)

    def _commit(self, tok, reads, writes):
        for r in reads:
            self.readers.setdefault(r, []).append(tok)
        for w in writes:
            self.lastw[w] = tok
            self.readers[w] = []

    def op(self, e, fn, reads=(), writes=()):
        self._deps(e, reads, writes)
        ins = fn(self.eng[e])
        self.ecnt[e] += 1
        ins.then_inc(self.esem[e], 1)
        self._commit(("e:" + e, self.ecnt[e]), reads, writes)
        self.nins += 1
        return ins

    def dma(self, e, slot, out, in_, reads=(), writes=(), **kw):
        sem = self._dslot(slot)
        self._deps(e, reads, writes)
        ins = self.eng[e].dma_start(out=out, in_=in_, **kw)
        self.dcnt[slot] += 16
        ins.then_inc(sem, 16)
        self._commit(("d:" + str(slot), self.dcnt[slot]), reads, writes)
        self.nins += 1
        return ins

    def barrier(self, rotate=False):
        for e in self.eng:
            for x in self.eng:
                if self.ecnt[x] > 0 and x != e:
                    self._wait(e, "e:" + x, self.ecnt[x])
            for sl, c in self.dcnt.items():
                if c > 0:
                    self._wait(e, "d:" + str(sl), c)
        if rotate:
            self.gen = getattr(self, "gen", 0) + 1
            old = dict(self.ecnt)
            olds = dict(self.esem)
            for e in self.eng:
                s = self.nc.semaphore("s_%s_%d" % (e, self.gen))
                self.esem[e] = s.__enter__()
                self._ctx.append(s)
                self.sems["e:" + e] = self.esem[e]
                self.ecnt[e] = 0
            self.lastw = {kk: vv for kk, vv in self.lastw.items() if vv[0].startswith("d:")}
            self.readers = {kk: [t for t in vv if t[0].startswith("d:")] for kk, vv in self.readers.items()}
            self.waited = {kk: vv for kk, vv in self.waited.items() if kk[1].startswith("d:")}

    def finish(self, keys, e="sp"):
        self._deps(e, list(keys), [])


class K:
    pass


def _bc(ap, shape):
    return ap.to_broadcast(list(shape))


def build(layer_ids, nblk=NBLK, out_from=None):
    nc = bass.Bass("TRN2", target_bir_lowering=False)
    k = K()
    k.nc = nc
    k.nblk = nblk
    S = Sched(nc)
    k.S = S
    din = lambda name, shape, dt=F32: nc.dram_tensor(name, list(shape), dt, kind="ExternalInput").ap()
    dint = lambda name, shape, dt=F32: nc.dram_tensor(name, list(shape), dt).ap()
    T = lambda name, shape, dt: nc.sbuf_tensor(name, list(shape), dt).__enter__()
    k.T = T
    I = {}
    din_pos = din if any(l % 3 == 2 for l in layer_ids) else (lambda *a, **kw: None)
    I["x"] = din("x", [L, D])
    I["positions"] = din_pos("positions", [L], I32)
    I["norm_pre"] = din("norm_pre", [4, D])
    I["norm_post"] = din("norm_post", [4, D])
    kinds = set(l % 3 for l in layer_ids)
    if 0 not in kinds:
        din_s5 = lambda *a, **kw: None
    else:
        din_s5 = din
    din_hg = din if 1 in kinds else (lambda *a, **kw: None)
    din_at = din if 2 in kinds else (lambda *a, **kw: None)
    I["s5_w_in"] = din_s5("s5_w_in", [2, D, 2048])
    I["s5_lambda_re"] = din_s5("s5_lambda_re", [2, 64, 64])
    I["s5_lambda_im"] = din_s5("s5_lambda_im", [2, 64, 64])
    I["s5_log_dt"] = din_s5("s5_log_dt", [2, 64])
    I["s5_b_re"] = din_s5("s5_b_re", [2, 64, 64, 16])
    I["s5_b_im"] = din_s5("s5_b_im", [2, 64, 64, 16])
    I["s5_c_re"] = din_s5("s5_c_re", [2, 64, 16, 64])
    I["s5_c_im"] = din_s5("s5_c_im", [2, 64, 16, 64])
    I["s5_d"] = din_s5("s5_d", [2, D])
    I["s5_w_glu"] = din_s5("s5_w_glu", [2, D, D])
    I["s5_b_glu"] = din_s5("s5_b_glu", [2, D])
    I["s5_w_out"] = din_s5("s5_w_out", [2, D, D])
    I["hg_w_in"] = din_hg("hg_w_in", [1, D, 4096])
    I["hg_lb_logits"] = din_hg("hg_lb_logits", [4, D])
    I["hg_norm"] = din_hg("hg_norm", [1, D])
    I["hg_w_out"] = din_hg("hg_w_out", [1, D, D])
    I["at_w_in"] = din_at("at_w_in", [1, D, 2560])
    I["at_b_in"] = din_at("at_b_in", [1, 1536])
    I["at_sinks"] = din_at("at_sinks", [1, 16])
    I["at_w_out"] = din_at("at_w_out", [1, D, D])
    I["c_ident"] = din("c_ident", [128, 128], BF16)
    I["c_f32"] = din("c_f32", [128, NCF])
    I["c_bf"] = din("c_bf", [128, 1024], BF16)
    k.I = I
    nc._in_names = [nm for nm, v in I.items() if v is not None]
    k.out = nc.dram_tensor("out", [L, D], F32, kind="ExternalOutput").ap()
    k.hbuf = [dint("hbuf0", [L, D]), dint("hbuf1", [L, D])]
    k.wbf = dint("wbf", [5, 128, 8, 1024], BF16)
    k.mats = dint("mats", [128, 64, 640], BF16)
    k.PS = [nc.psum_tensor("ps%d" % i, [128, 512], F32).__enter__() for i in range(8)]
    k.ident = T("ident", [128, 128], BF16)
    k.cf = T("cf", [128, NCF], F32)
    k.TM = [T("tm%d" % i, [128, 8, 1024], BF16) for i in range(3)]
    k.XT = T("xt", [128, 8, 8, 128], BF16)
    k.WS = [T("ws%d" % i, [128, 8, 1024], BF16) for i in range(2)]
    k.HS = [T("hs%d" % i, [128, 1024], F32) for i in range(3)]
    k.gpost = T("gpost", [128, 1024], F32)
    k.tmpf = [T("tmpf%d" % i, [128, 512], F32) for i in range(2)]
    k.tmpb = [T("tmpb%d" % i, [128, 512], BF16) for i in range(2)]
    k.junk = T("junk", [128, 1024], BF16)
    k.small = T("small", [128, 64], F32)
    k.ones_row = T("ones_row", [1, 128], BF16)
    k.brow = T("brow", [1, 1536], BF16)
    k.work = T("work", [128, 11136], F32)
    k.cnt = {"hs": 0, "ws": 0, "pp": 0, "tp": 0, "tf": 0, "tb": 0, "q": 0}

    S.dma("sp", "c0", k.ident[:], I["c_ident"], writes=["ident"])
    S.dma("sp", "c1", k.cf[:], I["c_f32"], writes=["cf"])
    S.op("pool", lambda e: e.memset(k.ones_row[:], 1.0), writes=["ones_row"])

    srcs = [I["x"]]
    n = len(layer_ids)
    for li in range(n):
        if li == n - 1:
            srcs.append(k.out)
        else:
            srcs.append(k.hbuf[li % 2])
    for li, lid in enumerate(layer_ids):
        kind, j = lid % 3, lid // 3
        if kind == 0:
            s5_layer(k, lid, j, srcs[li], srcs[li + 1])
        elif kind == 1:
            hg_layer(k, lid, j, srcs[li], srcs[li + 1])
        else:
            at_layer(k, lid, j, srcs[li], srcs[li + 1])
        S.barrier(rotate=True)
    S.finish([("hout", s) for s in range(8)] + ["hout"])
    S.barrier()
    return nc


def ps_bank(k, kind):
    if kind == "tp":
        b = k.cnt["tp"] % 2
        k.cnt["tp"] += 1
        return b
    if kind == "pp":
        b = 2 + k.cnt["pp"] % 4
        k.cnt["pp"] += 1
        return b
    raise ValueError


def prep_weight(k, w_ap, ncols, slot0, gpre_t=None):
    S = k.S
    wv = w_ap.rearrange("(kc p) n -> p kc n", p=128)
    PIECE = 256
    stage_f = [k.work[:, 0:2048].rearrange("p (a b) -> p a b", a=8), k.work[:, 2048:4096].rearrange("p (a b) -> p a b", a=8)]
    stage_b = [k.work[:, 4096:5120].bitcast(BF16).rearrange("p (a b) -> p a b", a=8),
               k.work[:, 5120:6144].bitcast(BF16).rearrange("p (a b) -> p a b", a=8)]
    for pi in range(ncols // PIECE):
        c0 = pi * PIECE
        q = k.cnt["q"] % 2
        k.cnt["q"] += 1
        S.dma("sp" if pi % 2 == 0 else "act", ("wst", q), stage_f[q], wv[:, :, c0:c0 + PIECE], writes=[("wstf", q)])
        if gpre_t is not None:
            S.op("dve", lambda e: e.tensor_tensor(out=stage_b[q], in0=stage_f[q], in1=_bc(gpre_t[:, :].unsqueeze(2), [128, 8, PIECE]), op=ALU.mult),
                 reads=[("wstf", q), "gpre"], writes=[("wstb", q)])
        else:
            S.op("act", lambda e: e.activation(out=stage_b[q], in_=stage_f[q], func=AF.Copy), reads=[("wstf", q)], writes=[("wstb", q)])
        slot = slot0 + c0 // 1024
        cc = c0 % 1024
        S.dma("pool", ("wsto", q), k.wbf[slot, :, :, cc:cc + PIECE], stage_b[q], reads=[("wstb", q)], writes=[("wbf", slot)])


def load_slot(k, slot, ncols=1024, eng="sp"):
    S = k.S
    i = k.cnt["ws"] % 2
    k.cnt["ws"] += 1
    S.dma(eng, ("ws", i), k.WS[i][:, :, 0:ncols], k.wbf[slot, :, :, 0:ncols], reads=[("wbf", slot)], writes=[("ws", i)])
    return i


def hview(h_ap, mode):
    if mode == "ci":
        return h_ap.rearrange("(b p s) d -> b s p d", p=128, s=8)
    return h_ap.rearrange("(b s p) d -> b s p d", p=128, s=8)


def norm_and_transpose(k, hv, blk):
    S = k.S
    for s in range(8):
        i = k.cnt["hs"] % 3
        k.cnt["hs"] += 1
        hs = k.HS[i]
        S.dma("sp" if s % 2 == 0 else "act", ("hs", i), hs[:], hv[blk, s], reads=["hsrc"], writes=[("hs", i)])
        S.op("act", lambda e: e.activation(out=k.junk[:], in_=hs[:], func=AF.Square, accum_out=k.small[:, s:s + 1]),
             reads=[("hs", i)], writes=["junk", ("ss", s)])
        S.op("act", lambda e: e.activation(out=k.small[:, 8 + s:9 + s], in_=k.small[:, s:s + 1], func=AF.Sqrt, scale=1.0 / D, bias=EPS),
             reads=[("ss", s)], writes=[("sq", s)])
        S.op("dve", lambda e: e.reciprocal(out=k.small[:, 16 + s:17 + s], in_=k.small[:, 8 + s:9 + s]), reads=[("sq", s)], writes=[("rs", s)])
        eng = "dve" if s % 2 == 0 else "pool"
        S.op(eng, lambda e: e.tensor_scalar(out=k.TM[0][:, s, :], in0=hs[:], scalar1=k.small[:, 16 + s:17 + s], scalar2=None, op0=ALU.mult),
             reads=[("hs", i), ("rs", s)], writes=[("tm0", s)])
    transpose_tm(k, 0)


def transpose_tm(k, ti):
    S = k.S
    for kc in range(8):
        b = ps_bank(k, "tp")
        pb = k.PS[b][:, :].bitcast(BF16).rearrange("p (s t) -> p s t", s=8)
        for s in range(8):
            S.op("pe", lambda e: e.transpose(out=pb[:, s, :], in_=k.TM[ti][:, s, kc * 128:(kc + 1) * 128], identity=k.ident[:]),
                 reads=[("tm%d" % ti, s), "ident"], writes=[("ps", b)])
        if kc % 2 == 0:
            S.op("act", lambda e: e.activation(out=k.XT[:, kc, :, :], in_=pb, func=AF.Copy), reads=[("ps", b)], writes=[("xt", kc)])
        else:
            S.op("dve", lambda e: e.tensor_copy(out=k.XT[:, kc, :, :], in_=pb), reads=[("ps", b)], writes=[("xt", kc)])


def project(k, wslot_i, ncols, evac, bias_cols=None):
    S = k.S
    W = k.WS[wslot_i]
    for s in range(8):
        for half in range(ncols // 512):
            b = ps_bank(k, "pp")
            for kc in range(8):
                S.op("pe", lambda e: e.matmul(k.PS[b][:, :], lhsT=k.XT[:, kc, s, :], rhs=W[:, kc, half * 512:(half + 1) * 512],
                                              start=(kc == 0), stop=(kc == 7 and (bias_cols is None or bias_cols(half) is None))),
                     reads=[("xt", kc), ("ws", wslot_i)], writes=[("ps", b)])
            if bias_cols is not None and bias_cols(half) is not None:
                o = bias_cols(half)
                S.op("pe", lambda e: e.matmul(k.PS[b][:, :], lhsT=k.ones_row[:, :], rhs=k.brow[:, o:o + 512], start=False, stop=True),
                     reads=["ones_row", "brow"], writes=[("ps", b)])
            evac(s, half, b)


def out_proj_residual(k, wslot_i, hv_in, hv_out, blk):
    S = k.S
    W = k.WS[wslot_i]
    for s in range(8):
        i = k.cnt["hs"] % 3
        k.cnt["hs"] += 1
        hs = k.HS[i]
        S.dma("sp" if s % 2 == 0 else "act", ("hs", i), hs[:], hv_in[blk, s], reads=["hsrc"], writes=[("hs", i)])
        banks = []
        for half in range(2):
            b = ps_bank(k, "pp")
            banks.append(b)
            for kc in range(8):
                S.op("pe", lambda e: e.matmul(k.PS[b][:, :], lhsT=k.XT[:, kc, s, :], rhs=W[:, kc, half * 512:(half + 1) * 512],
                                              start=(kc == 0), stop=(kc == 7)),
                     reads=[("xt", kc), ("ws", wslot_i)], writes=[("ps", b)])
            S.op("act", lambda e: e.activation(out=k.junk[:, 0:512], in_=k.PS[b][:, :], func=AF.Square, accum_out=k.small[:, 24 + half:25 + half]),
                 reads=[("ps", b)], writes=["junk", ("os", half)])
        S.op("dve", lambda e: e.tensor_tensor(out=k.small[:, 26:27], in0=k.small[:, 24:25], in1=k.small[:, 25:26], op=ALU.add),
             reads=[("os", 0), ("os", 1)], writes=["os2"])
        S.op("act", lambda e: e.activation(out=k.small[:, 27:28], in_=k.small[:, 26:27], func=AF.Sqrt, scale=1.0 / D, bias=EPS), reads=["os2"], writes=["os3"])
        S.op("dve", lambda e: e.reciprocal(out=k.small[:, 28:29], in_=k.small[:, 27:28]), reads=["os3"], writes=["os4"])
        for half in range(2):
            b = banks[half]
            tf = k.cnt["tf"] % 2
            k.cnt["tf"] += 1
            S.op("dve", lambda e: e.scalar_tensor_tensor(out=k.tmpf[tf][:, :], in0=k.PS[b][:, :], scalar=k.small[:, 28:29], in1=k.gpost[:, half * 512:(half + 1) * 512],
                                                        op0=ALU.mult, op1=ALU.mult),
                 reads=[("ps", b), "os4", "gpost"], writes=[("tmpf", tf)])
            S.op("pool", lambda e: e.tensor_tensor(out=hs[:, half * 512:(half + 1) * 512], in0=hs[:, half * 512:(half + 1) * 512], in1=k.tmpf[tf][:, :], op=ALU.add),
                 reads=[("tmpf", tf), ("hs", i)], writes=[("hs", i)])
        S.dma("pool", ("hso", i), hv_out[blk, s], hs[:], reads=[("hs", i)], writes=["hout", ("hout", s)])


def layer_common_setup(k, lid):
    S = k.S
    k.gpre = k.small[:, 32:40]
    S.dma("sp", "g0", k.gpre, k.I["norm_pre"][lid].rearrange("(kc p) -> p kc", p=128), writes=["gpre"], allow_slow_non_contiguous=True)
    S.dma("act", "g1", k.gpost[:], k.I["norm_post"][lid].partition_broadcast(128), writes=["gpost"])


NCF = 1184


def host_consts_bf():
    kk = np.arange(128)[:, None]
    qq = np.arange(128)[None, :]
    cur = np.where(kk <= qq, 0.0, -30000.0).astype(np.float32)
    prv = np.where(kk > qq, 0.0, -30000.0).astype(np.float32)
    c = np.concatenate([np.tile(cur, (1, 4)), np.tile(prv, (1, 4))], axis=1)
    return c.astype(ml_dtypes.bfloat16)


def host_consts():
    c = np.zeros((128, NCF), np.float32)
    c[:, 0:129] = 8.0 * np.arange(129, dtype=np.float32)[None, :]
    jv = np.concatenate([-np.arange(8), 7 - np.arange(8), np.arange(8), 1 + np.arange(8)]).astype(np.float32)
    c[:, 136:168] = jv[None, :]
    c[:64, 170] = -1.0
    c[64:, 170] = 1.0
    c[:64, 171] = 1.0
    c[64:, 171] = -1.0
    ii = np.arange(128) // 16
    c[:, 176:304] = (ii[:, None] <= ii[None, :]).astype(np.float32)
    for j in range(64):
        c[64 + j, 304 + j] = -1.0
        c[j, 304 + 64 + j] = 1.0
    c[:, 432:464] = (10000.0 ** (-np.arange(0, 64, 2, dtype=np.float32) / 64.0)).astype(np.float32)[None, :]
    c[:, 470] = np.arange(128, dtype=np.float32)
    rm = np.ones(512, np.float32)
    rm[::64] = 0.0
    c[:, 512:1024] = rm[None, :]
    s_ = np.arange(128)
    c[:, 1024:1152] = ((s_[:, None] // 64 == s_[None, :] // 64) & (s_[:, None] <= s_[None, :])).astype(np.float32)
    c[:, 1152:1160] = (np.arange(8) % 2 == 0).astype(np.float32)[None, :]
    c[:, 1160:1168] = (np.arange(8) % 2 == 1).astype(np.float32)[None, :]
    return c


def sin_of_g(k, ang, shift, kk, ki, dst, rk, skey, dkey):
    S = k.S
    C1 = 6.28125
    C2 = TWO_PI - 6.28125
    kf = ki.bitcast(F32)

    def dv(fn, reads, writes):
        S.op("dve", fn, reads=reads, writes=writes)
    dv(lambda e: e.tensor_scalar(out=kk, in0=ang, scalar1=shift, scalar2=1.0 / TWO_PI, op0=ALU.add, op1=ALU.mult), rk, [skey])
    dv(lambda e: e.tensor_copy(out=ki, in_=kk), [skey], [skey])
    dv(lambda e: e.tensor_copy(out=kk, in_=ki), [skey], [skey])
    dv(lambda e: e.scalar_tensor_tensor(out=kf, in0=kk, scalar=-C1, in1=ang, op0=ALU.mult, op1=ALU.add), rk + [skey], [skey])
    dv(lambda e: e.scalar_tensor_tensor(out=kf, in0=kk, scalar=-C2, in1=kf, op0=ALU.mult, op1=ALU.add), [skey], [skey])
    dv(lambda e: e.tensor_scalar(out=kf, in0=kf, scalar1=shift, scalar2=math.pi, op0=ALU.add, op1=ALU.min), [skey], [skey])
    dv(lambda e: e.tensor_scalar(out=kf, in0=kf, scalar1=-math.pi, scalar2=None, op0=ALU.max), [skey], [skey])
    S.op("act", lambda e: e.activation(out=dst, in_=kf, func=AF.Sin), reads=[skey], writes=[dkey])


def s5_layer(k, lid, j, h_in, h_out):
    S = k.S
    I = k.I
    cf = k.cf
    layer_common_setup(k, lid)
    prep_weight(k, I["s5_w_in"][j], 2048, 0, gpre_t=k.gpre)
    prep_weight(k, I["s5_w_glu"][j], 1024, 2)
    prep_weight(k, I["s5_w_out"][j], 1024, 3)
    S.barrier()
    browf = k.work[0:1, 0:1024]
    S.dma("sp", "b0", browf, I["s5_b_glu"][j].unsqueeze(0), writes=["browf"])
    S.op("dve", lambda e: e.tensor_copy(out=k.brow[:, 0:1024], in_=browf), reads=["browf"], writes=["brow"])
    S.barrier()
    s5_tables(k, j)
    S.barrier()
    hv_in = hview(h_in, "ci")
    hv_out = hview(h_out, "ci")
    W = k.work
    def carve(off, n, dt=F32):
        v = W[:, off:off + n]
        return v if dt == F32 else v.bitcast(dt)
    MQ = [carve(0, 1280, BF16).rearrange("p (g m) -> p g m", g=4), carve(1280, 1280, BF16).rearrange("p (g m) -> p g m", g=4)]
    UGq = [carve(2560, 256, BF16).rearrange("p (g m) -> p g m", g=4), carve(2816, 256, BF16).rearrange("p (g m) -> p g m", g=4)]
    tA = [carve(3072, 512).rearrange("p (g m) -> p g m", g=4), carve(3584, 512).rearrange("p (g m) -> p g m", g=4)]
    tB = [carve(4096, 512).rearrange("p (g m) -> p g m", g=4), carve(4608, 512).rearrange("p (g m) -> p g m", g=4)]
    ZT = [carve(5120, 512).rearrange("p (g m) -> p g m", g=4), carve(5632, 512).rearrange("p (g m) -> p g m", g=4)]
    ST = [carve(6144, 516).rearrange("p (g m) -> p g m", g=4), carve(6660, 516).rearrange("p (g m) -> p g m", g=4)]
    P1 = [carve(7176, 256, BF16).rearrange("p (g m) -> p g m", g=4), carve(7432, 256, BF16).rearrange("p (g m) -> p g m", g=4)]
    P2 = [carve(7688, 256, BF16).rearrange("p (g m) -> p g m", g=4), carve(7944, 256, BF16).rearrange("p (g m) -> p g m", g=4)]
    LAST = carve(8200, 64)
    CARRY = carve(8264, 64)
    CT = carve(8328, 64)
    COS = k.s5_COS
    SIN = k.s5_SIN
    RHO = k.s5_RHO
    C128 = k.s5_C128
    S128 = k.s5_S128
    Jm = cf[:, 304:432]
    S.op("dve", lambda e: e.memset(CARRY, 0.0), writes=["carry"])
    UT = k.TM[1][:, :, :].rearrange("p s (g h) -> p g s h", h=16)
    UTp = k.TM[1][:, :, :].rearrange("p a b -> p (a b)").rearrange("p (g i h) -> p g i h", g=64, i=8)
    for blk in range(k.nblk):
        norm_and_transpose(k, hv_in, blk)
        wi = load_slot(k, 0)

        def evac_u(s, half, b):
            src = k.PS[b][:, :].rearrange("p (g h) -> p g h", h=16)
            dst = UTp[:, half * 32:(half + 1) * 32, s, :]
            if (s + half) % 2 == 0:
                S.op("act", lambda e: e.activation(out=dst, in_=src, func=AF.Copy), reads=[("ps", b)], writes=[("tm1", "u")])
            else:
                S.op("dve", lambda e: e.tensor_copy(out=dst, in_=src), reads=[("ps", b)], writes=[("tm1", "u")])
        project(k, wi, 1024, evac_u)
        wi = load_slot(k, 1, eng="act")

        def evac_z(s, half, b):
            S.op("act", lambda e: e.activation(out=k.TM[2][:, s, half * 512:(half + 1) * 512], in_=k.PS[b][:, :], func=AF.Silu),
                 reads=[("ps", b)], writes=[("tm2", s)])
        project(k, wi, 1024, evac_z)
        for q in range(16):
            g0 = 4 * q
            qb = q % 2
            S.dma("sp" if q % 2 == 0 else "act", ("mq", qb), MQ[qb], k.mats[:, g0:g0 + 4, :], reads=["mats"], writes=[("mq", qb)])
            b = ps_bank(k, "tp")
            pb = k.PS[b][:, 0:256].bitcast(BF16).rearrange("p (g t) -> p g t", g=4)
            for jj in range(4):
                S.op("pe", lambda e: e.transpose(out=pb[:, jj, :], in_=UTp[:, g0 + jj, :, :].rearrange("p i h -> p (i h)"), identity=k.ident[:]),
                     reads=[("tm1", "u"), "ident"], writes=[("ps", b)])
            S.op("act", lambda e: e.activation(out=UGq[qb], in_=pb, func=AF.Copy), reads=[("ps", b)], writes=[("ugq", qb)])
            for jj in range(4):
                S.op("pe", lambda e: e.matmul(k.PS[6][:, jj * 128:(jj + 1) * 128], lhsT=MQ[qb][:, jj, 0:128], rhs=UGq[qb][:, jj, :], start=True, stop=True),
                     reads=[("mq", qb), ("ugq", qb)], writes=[("ps", 6)])
                S.op("pe", lambda e: e.matmul(k.PS[7][:, jj * 128:(jj + 1) * 128], lhsT=MQ[qb][:, jj, 128:256], rhs=UGq[qb][:, jj, :], start=True, stop=True),
                     reads=[("mq", qb), ("ugq", qb)], writes=[("ps", 7)])
            pz1 = k.PS[6][:, :].rearrange("p (g t) -> p g t", g=4)
            pz2 = k.PS[7][:, :].rearrange("p (g t) -> p g t", g=4)
            S.op("dve", lambda e: e.tensor_tensor(out=tA[qb], in0=pz1, in1=COS[:, g0:g0 + 4, 1:129], op=ALU.mult), reads=[("ps", 6), "s5tab"], writes=[("tA", qb)])
            S.op("dve", lambda e: e.tensor_tensor(out=tB[qb], in0=pz2, in1=SIN[:, g0:g0 + 4, 1:129], op=ALU.mult), reads=[("ps", 7), "s5tab"], writes=[("tB", qb)])
            S.op("pool", lambda e: e.tensor_tensor(out=ZT[qb], in0=tA[qb], in1=tB[qb], op=ALU.add), reads=[("tA", qb), ("tB", qb)], writes=[("ZT", qb)])
            S.op("pool", lambda e: e.tensor_copy(out=ST[qb][:, :, 0:1], in_=CARRY[:, g0:g0 + 4].unsqueeze(2)), reads=["carry"], writes=[("ST", qb)])
            for jj in range(4):
                S.op("dve", lambda e: e.tensor_tensor_scan(out=ST[qb][:, jj, 1:129], data0=_bc(RHO[:, g0 + jj:g0 + jj + 1], [128, 128]), data1=ZT[qb][:, jj, :],
                                                           initial=ST[qb][:, jj, 0:1], op0=ALU.mult, op1=ALU.add),
                     reads=[("ZT", qb), ("ST", qb), "s5tab"], writes=[("ST", qb)])
            S.op("pool", lambda e: e.tensor_tensor(out=P1[qb], in0=ST[qb][:, :, 0:128], in1=COS[:, g0:g0 + 4, 0:128], op=ALU.mult), reads=[("ST", qb), "s5tab"], writes=[("P1", qb)])
            S.op("dve", lambda e: e.tensor_tensor(out=P2[qb], in0=ST[qb][:, :, 0:128], in1=SIN[:, g0:g0 + 4, 0:128], op=ALU.mult), reads=[("ST", qb), "s5tab"], writes=[("P2", qb)])
            S.op("act", lambda e: e.activation(out=LAST[:, g0:g0 + 4].unsqueeze(2), in_=ST[qb][:, :, 128:129], func=AF.Copy), reads=[("ST", qb)], writes=["last"])
            yb = ps_bank(k, "pp")
            for jj in range(4):
                o = k.PS[yb][:, jj * 128:(jj + 1) * 128]
                S.op("pe", lambda e: e.matmul(o, lhsT=UGq[qb][:, jj, :], rhs=MQ[qb][:, jj, 256:384], start=True, stop=False), reads=[("ugq", qb), ("mq", qb)], writes=[("ps", yb)])
                S.op("pe", lambda e: e.matmul(o, lhsT=P1[qb][:, jj, :], rhs=MQ[qb][:, jj, 384:512], start=False, stop=False), reads=[("P1", qb), ("mq", qb)], writes=[("ps", yb)])
                S.op("pe", lambda e: e.matmul(o, lhsT=P2[qb][:, jj, :], rhs=MQ[qb][:, jj, 512:640], start=False, stop=True), reads=[("P2", qb), ("mq", qb)], writes=[("ps", yb)])
            for jj in range(4):
                src = k.PS[yb][:, jj * 128:(jj + 1) * 128].rearrange("p (i h) -> p i h", h=16)
                dst = k.TM[0][:, :, 16 * (g0 + jj):16 * (g0 + jj) + 16]
                S.op("act", lambda e: e.activation(out=dst, in_=src, func=AF.Gelu_apprx_tanh), reads=[("ps", yb)], writes=[("tm0", s) for s in range(8)])
        S.op("pe", lambda e: e.matmul(k.PS[6][:, 0:64], lhsT=Jm, rhs=LAST, start=True, stop=True), reads=["last", "cf"], writes=[("ps", 6)])
        S.op("dve", lambda e: e.tensor_tensor(out=CT, in0=k.PS[6][:, 0:64], in1=S128, op=ALU.mult), reads=[("ps", 6), "s5tab"], writes=["ct"])
        S.op("dve", lambda e: e.tensor_tensor(out=CARRY, in0=LAST, in1=C128, op=ALU.mult), reads=["last", "s5tab"], writes=["carry"])
        S.op("dve", lambda e: e.tensor_tensor(out=CARRY, in0=CARRY, in1=CT, op=ALU.add), reads=["ct", "carry"], writes=["carry"])
        transpose_tm(k, 0)
        wi = load_slot(k, 2)

        def evac_glu(s, half, b):
            tb = k.cnt["tb"] % 2
            k.cnt["tb"] += 1
            cs = slice(half * 512, (half + 1) * 512)
            S.op("act", lambda e: e.activation(out=k.tmpb[tb][:, :], in_=k.PS[b][:, :], func=AF.Sigmoid), reads=[("ps", b)], writes=[("tmpb", tb)])
            S.op("dve", lambda e: e.tensor_tensor(out=k.tmpb[tb][:, :], in0=k.tmpb[tb][:, :], in1=k.TM[0][:, s, cs], op=ALU.mult),
                 reads=[("tmpb", tb), ("tm0", s)], writes=[("tmpb", tb)])
            S.op("pool", lambda e: e.tensor_tensor(out=k.TM[2][:, s, cs], in0=k.TM[2][:, s, cs], in1=k.tmpb[tb][:, :], op=ALU.mult),
                 reads=[("tmpb", tb), ("tm2", s)], writes=[("tm2", s)])
        project(k, wi, 1024, evac_glu, bias_cols=lambda half: half * 512)
        transpose_tm(k, 2)
        wi = load_slot(k, 3, eng="act")
        out_proj_residual(k, wi, hv_in, hv_out, blk)


def s5_tables(k, j):
    S = k.S
    I = k.I
    cf = k.cf
    T = k.T
    if not hasattr(k, "s5_COS"):
        k.s5_COS = T("s5cos", [128, 64, 129], BF16)
        k.s5_SIN = T("s5sin", [128, 64, 129], BF16)
        k.s5_misc = T("s5misc", [128, 3, 64], F32)
    k.s5_RHO = k.s5_misc[:, 0, :]
    k.s5_C128 = k.s5_misc[:, 1, :]
    k.s5_S128 = k.s5_misc[:, 2, :]
    W = k.work
    o = [0]

    def alloc(n):
        v = W[:, o[0]:o[0] + n]
        o[0] += n
        return v

    def dv(fn, reads, writes, eng="dve"):
        S.op(eng, fn, reads=reads, writes=writes)
    LR = alloc(64); LI = alloc(64); DT = alloc(64); LRDT = alloc(64); LIDT = alloc(64)
    AR = alloc(64); AI = alloc(64); MAG = alloc(64); KK = alloc(64); KI = alloc(64).bitcast(I32)
    DEN = alloc(64); T1 = alloc(64); T2 = alloc(64); Q1 = alloc(64); Q2 = alloc(64); AM1 = alloc(64); A128 = alloc(64)
    DG = alloc(64)
    base = o[0]
    lre = I["s5_lambda_re"][j].rearrange("g p -> p g")
    lim = I["s5_lambda_im"][j].rearrange("g p -> p g")
    for hf in range(2):
        S.dma("sp", "t0", LR[hf * 64:(hf + 1) * 64, :], lre, writes=["LR"], allow_slow_non_contiguous=True)
        S.dma("act", "t1", LI[hf * 64:(hf + 1) * 64, :], lim, writes=["LI"], allow_slow_non_contiguous=True)
    S.dma("pool", "t2", DT, I["s5_log_dt"][j].partition_broadcast(128), writes=["DT"])
    S.op("act", lambda e: e.activation(out=DT, in_=DT, func=AF.Exp), reads=["DT"], writes=["DT"])
    dv(lambda e: e.tensor_tensor(out=LRDT, in0=LR, in1=DT, op=ALU.mult), ["LR", "DT"], ["LRDT"])
    dv(lambda e: e.tensor_tensor(out=LIDT, in0=LI, in1=DT, op=ALU.mult), ["LI", "DT"], ["LIDT"])

    def sin_of(ang, shift, kk, ki, dst, rk, skey, dkey):
        sin_of_g(k, ang, shift, kk, ki, dst, rk, skey, dkey)

    SGN = cf[:, 170:171]
    NS = cf[:, 171:172]
    S.op("act", lambda e: e.activation(out=MAG, in_=LRDT, func=AF.Exp), reads=["LRDT"], writes=["MAG"])
    sin_of(LIDT, 0.0, KK, KI, AI, ["LIDT"], "sk0", "AI")
    dv(lambda e: e.tensor_tensor(out=AI, in0=AI, in1=MAG, op=ALU.mult), ["AI", "MAG"], ["AI"])
    sin_of(LIDT, math.pi / 2, KK, KI, AR, ["LIDT"], "sk0", "AR")
    dv(lambda e: e.tensor_tensor(out=AR, in0=AR, in1=MAG, op=ALU.mult), ["AR", "MAG"], ["AR"])
    dv(lambda e: e.tensor_tensor(out=DEN, in0=LR, in1=LR, op=ALU.mult), ["LR"], ["DEN"])
    dv(lambda e: e.tensor_tensor(out=T1, in0=LI, in1=LI, op=ALU.mult), ["LI"], ["T1"])
    dv(lambda e: e.tensor_tensor(out=DEN, in0=DEN, in1=T1, op=ALU.add), ["DEN", "T1"], ["DEN"])
    dv(lambda e: e.reciprocal(out=DEN, in_=DEN), ["DEN"], ["DEN"])
    dv(lambda e: e.tensor_scalar(out=AM1, in0=AR, scalar1=-1.0, scalar2=None, op0=ALU.add), ["AR"], ["AM1"])
    dv(lambda e: e.tensor_tensor(out=T1, in0=AM1, in1=LR, op=ALU.mult), ["AM1", "LR", "T1"], ["T1"])
    dv(lambda e: e.tensor_tensor(out=T2, in0=AI, in1=LI, op=ALU.mult), ["AI", "LI"], ["T2"])
    dv(lambda e: e.tensor_tensor(out=T1, in0=T1, in1=T2, op=ALU.add), ["T1", "T2"], ["T1"])
    dv(lambda e: e.tensor_tensor(out=Q1, in0=T1, in1=DEN, op=ALU.mult), ["T1", "DEN"], ["Q1"])
    dv(lambda e: e.tensor_tensor(out=T1, in0=AI, in1=LR, op=ALU.mult), ["AI", "LR", "T1"], ["T1"])
    dv(lambda e: e.tensor_tensor(out=T2, in0=AM1, in1=LI, op=ALU.mult), ["AM1", "LI", "T2"], ["T2"])
    dv(lambda e: e.tensor_tensor(out=T1, in0=T1, in1=T2, op=ALU.subtract), ["T1", "T2"], ["T1"])
    dv(lambda e: e.tensor_tensor(out=T1, in0=T1, in1=DEN, op=ALU.mult), ["T1", "DEN"], ["T1"])
    dv(lambda e: e.tensor_scalar(out=Q2, in0=T1, scalar1=SGN, scalar2=None, op0=ALU.mult), ["T1", "cf"], ["Q2"])
    S.op("act", lambda e: e.activation(out=k.s5_RHO, in_=LRDT, func=AF.Exp, scale=8.0), reads=["LRDT"], writes=["s5tab"])
    dv(lambda e: e.tensor_scalar(out=A128, in0=LIDT, scalar1=1024.0, scalar2=None, op0=ALU.mult), ["LIDT"], ["A128"])
    sin_of(A128, 0.0, KK, KI, k.s5_S128, ["A128"], "sk0", "s5tab")
    sin_of(A128, math.pi / 2, KK, KI, k.s5_C128, ["A128"], "sk0", "s5tab")
    dsrc = I["s5_d"][j].rearrange("(g h) -> h g", h=16)
    for i in range(8):
        S.dma("sp" if i % 2 == 0 else "act", "t7", DG[16 * i:16 * i + 16, :], dsrc, writes=["DG"], allow_slow_non_contiguous=True)
    ramp = cf[:, 0:129]
    o[0] = base
    ANG = alloc(8 * 129).rearrange("p (g c) -> p g c", g=8)
    KK2 = alloc(8 * 129).rearrange("p (g c) -> p g c", g=8)
    KI2 = alloc(8 * 129).bitcast(I32).rearrange("p (g c) -> p g c", g=8)
    for pc in range(8):
        gs = slice(pc * 8, pc * 8 + 8)
        dv(lambda e: e.tensor_tensor(out=ANG, in0=_bc(LIDT[:, gs].unsqueeze(2), [128, 8, 129]), in1=_bc(ramp.unsqueeze(1), [128, 8, 129]), op=ALU.mult),
           ["LIDT", "cf", "sk1"], ["ANG"])
        sin_of(ANG, 0.0, KK2, KI2, k.s5_SIN[:, gs, :], ["ANG"], "sk1", "s5tab")
        sin_of(ANG, math.pi / 2, KK2, KI2, k.s5_COS[:, gs, :], ["ANG"], "sk1", "s5tab")
    S.barrier()
    o[0] = base
    jv = cf[:, 136:168]
    PW1 = alloc(2048).rearrange("p (g j) -> p g j", g=64)
    PW2 = alloc(2048).rearrange("p (g j) -> p g j", g=64)
    pbase = o[0]
    KI3 = alloc(2048).bitcast(I32).rearrange("p (g j) -> p g j", g=64)
    MG3 = alloc(2048).rearrange("p (g j) -> p g j", g=64)
    xtf = k.XT[:, :, :, :].rearrange("p a b c -> p (a b c)").bitcast(F32)
    ANG3 = xtf[:, 0:2048].rearrange("p (g j) -> p g j", g=64)
    KK3 = xtf[:, 2048:4096].rearrange("p (g j) -> p g j", g=64)
    dv(lambda e: e.tensor_tensor(out=ANG3, in0=_bc(LIDT.unsqueeze(2), [128, 64, 32]), in1=_bc(jv.unsqueeze(1), [128, 64, 32]), op=ALU.mult), ["LIDT", "cf"], ["ANG3"])
    dv(lambda e: e.tensor_tensor(out=MG3, in0=_bc(LRDT.unsqueeze(2), [128, 64, 32]), in1=_bc(jv.unsqueeze(1), [128, 64, 32]), op=ALU.mult), ["LRDT", "cf"], ["MG3"])
    S.op("act", lambda e: e.activation(out=MG3, in_=MG3, func=AF.Exp), reads=["MG3"], writes=["MG3"])
    sin_of(ANG3, math.pi / 2, KK3, KI3, PW1, ["ANG3"], "sk2", "PW1")
    dv(lambda e: e.tensor_tensor(out=PW1, in0=PW1, in1=MG3, op=ALU.mult), ["PW1", "MG3"], ["PW1"])
    sin_of(ANG3, 0.0, KK3, KI3, PW2, ["ANG3"], "sk2", "PW2")
    dv(lambda e: e.tensor_tensor(out=PW2, in0=PW2, in1=MG3, op=ALU.mult), ["PW2", "MG3"], ["PW2"])
    dv(lambda e: e.tensor_scalar(out=PW2, in0=PW2, scalar1=SGN, scalar2=None, op0=ALU.mult), ["PW2", "cf"], ["PW2"])
    S.barrier()
    o[0] = pbase
    G3 = lambda v: v.rearrange("p (g h) -> p g h", g=64)
    Cs1 = G3(xtf[:, 0:1024]); Cs2 = G3(xtf[:, 1024:2048]); Bs1 = G3(xtf[:, 2048:3072]); Bs2 = G3(xtf[:, 3072:4096])
    CC1 = alloc(1024).rearrange("p (t c) -> p t c", t=8)
    CC2 = alloc(1024).rearrange("p (t c) -> p t c", t=8)
    Bb1 = G3(alloc(1024)); Bb2 = G3(alloc(1024)); TT = G3(alloc(1024))
    assert o[0] <= W.shape[1], o[0]
    tm0f = k.TM[0][:, :, :].rearrange("p a b -> p (a b)")[:, 4096:8192].bitcast(F32)
    E1 = G3(tm0f[:, 0:1024]); E2 = G3(tm0f[:, 1024:2048])
    tm2f = k.TM[2][:, :, :].rearrange("p a b -> p (a b)")[:, 5120:8192].bitcast(F32)
    Cn1 = G3(tm2f[:, 0:1024])
    scr = k.TM[1][:, :, :].rearrange("p a b -> p (a b)").bitcast(F32)
    PA = scr[:, 0:1024].rearrange("p (g i h) -> p g i h", g=8, i=8)
    PB = scr[:, 1024:2048].rearrange("p (g i h) -> p g i h", g=8, i=8)
    identf = scr[:, 2048:2176]
    Cn2 = G3(scr[:, 2176:3200])
    F1 = G3(CC1.rearrange("p t c -> p (t c)"))
    S.op("dve", lambda e: e.tensor_copy(out=identf, in_=k.ident[:]), reads=["ident"], writes=["identf"])
    bre = I["s5_b_re"][j].rearrange("g p h -> p g h")
    bim = I["s5_b_im"][j].rearrange("g p h -> p g h")
    S.dma("sp", "t3", Bs1[0:64], bre, writes=["Bs1"])
    S.dma("act", "t3", Bs1[64:128], bim, writes=["Bs1"])
    S.dma("sp", "t4", Bs2[0:64], bim, writes=["Bs2"])
    S.dma("act", "t4", Bs2[64:128], bre, writes=["Bs2"])
    cre = I["s5_c_re"][j].rearrange("(t g) h p -> (g h) t p", t=8)
    cim = I["s5_c_im"][j].rearrange("(t g) h p -> (g h) t p", t=8)
    S.dma("sp", "t5", CC1[:, :, 0:64], cre, writes=["CC1"])
    S.dma("act", "t5", CC1[:, :, 64:128], cim, writes=["CC1"])
    S.dma("sp", "t6", CC2[:, :, 0:64], cim, writes=["CC2"])
    S.dma("act", "t6", CC2[:, :, 64:128], cre, writes=["CC2"])
    for (CC, Cs, ck, sk) in ((CC1, Cs1, "CC1", "Cs1"), (CC2, Cs2, "CC2", "Cs2")):
        for t in range(8):
            b = ps_bank(k, "pp")
            S.op("pe", lambda e: e.transpose(out=k.PS[b][:, 0:128], in_=CC[:, t, :], identity=identf), reads=[ck, "identf"], writes=[("ps", b)])
            S.op("act", lambda e: e.activation(out=Cs[:, 8 * t:8 * t + 8, :], in_=k.PS[b][:, 0:128].rearrange("p (g h) -> p g h", g=8), func=AF.Copy),
                 reads=[("ps", b)], writes=[sk])
    Q1b = _bc(Q1.unsqueeze(2), [128, 64, 16]); Q2b = _bc(Q2.unsqueeze(2), [128, 64, 16])
    dv(lambda e: e.tensor_tensor(out=Bb1, in0=Bs1, in1=Q1b, op=ALU.mult), ["Bs1", "Q1"], ["Bb1"])
    dv(lambda e: e.tensor_tensor(out=TT, in0=Bs2, in1=Q2b, op=ALU.mult), ["Bs2", "Q2"], ["TT"])
    dv(lambda e: e.tensor_tensor(out=Bb1, in0=Bb1, in1=TT, op=ALU.add), ["Bb1", "TT"], ["Bb1"])
    dv(lambda e: e.tensor_tensor(out=Bb2, in0=Bs2, in1=Q1b, op=ALU.mult), ["Bs2", "Q1"], ["Bb2"])
    dv(lambda e: e.tensor_tensor(out=TT, in0=Bs1, in1=Q2b, op=ALU.mult), ["Bs1", "Q2", "TT"], ["TT"])
    dv(lambda e: e.tensor_tensor(out=Bb2, in0=Bb2, in1=TT, op=ALU.subtract), ["Bb2", "TT"], ["Bb2"])
    dv(lambda e: e.tensor_scalar(out=E1, in0=Bb2, scalar1=NS, scalar2=None, op0=ALU.mult), ["Bb2", "cf"], ["E1"])
    dv(lambda e: e.tensor_scalar(out=E2, in0=Bb1, scalar1=NS, scalar2=-1.0, op0=ALU.mult, op1=ALU.mult), ["Bb1", "cf"], ["E2"])
    dv(lambda e: e.tensor_scalar(out=Cn1, in0=Cs1, scalar1=NS, scalar2=None, op0=ALU.mult), ["Cs1", "cf"], ["Cn1"])
    dv(lambda e: e.tensor_scalar(out=Cn2, in0=Cs2, scalar1=NS, scalar2=None, op0=ALU.mult), ["Cs2", "cf"], ["Cn2"])
    dv(lambda e: e.tensor_scalar(out=F1, in0=Cs2, scalar1=-1.0, scalar2=None, op0=ALU.mult), ["Cs2", "CC1"], ["F1", "CC1"])
    F2 = Cs1
    mask = cf[:, 176:304]
    bsc = k.TM[0][:, :, :].rearrange("p a b -> p (a b)")
    OUTM = k.TM[2][:, :, :].rearrange("p a b -> p (a b)")[:, 0:8 * 640].rearrange("p (g m) -> p g m", g=8)
    TG1 = bsc[:, 0:1024].rearrange("p (g m) -> p g m", g=8)
    TG2 = bsc[:, 1024:2048].rearrange("p (g m) -> p g m", g=8)
    X1 = bsc[:, 2048:3072].rearrange("p (g m) -> p g m", g=8)
    X1b = bsc[:, 3072:4096].rearrange("p (g m) -> p g m", g=8)
    allk = ["PW1", "PW2", "Bb1", "Bb2", "E1", "E2", "Cn1", "Cn2", "F1", "Cs1"]

    def prod(dst, pj0, A1, A2, gs, tag):
        d4 = dst.rearrange("p g (i h) -> p g i h", i=8)
        pw1 = _bc(PW1[:, gs, pj0:pj0 + 8].unsqueeze(3), [128, 8, 8, 16])
        pw2 = _bc(PW2[:, gs, pj0:pj0 + 8].unsqueeze(3), [128, 8, 8, 16])
        a1 = _bc(A1[:, gs, :].unsqueeze(2), [128, 8, 8, 16])
        a2 = _bc(A2[:, gs, :].unsqueeze(2), [128, 8, 8, 16])
        dv(lambda e: e.tensor_tensor(out=PA, in0=pw1, in1=a1, op=ALU.mult), allk, ["PA"])
        dv(lambda e: e.tensor_tensor(out=PB, in0=pw2, in1=a2, op=ALU.mult), allk, ["PB"], eng="pool")
        dv(lambda e: e.tensor_tensor(out=d4, in0=PA, in1=PB, op=ALU.add), ["PA", "PB"], [tag])
    for oc in range(8):
        gs = slice(oc * 8, oc * 8 + 8)
        prod(TG1, 0, Bb1, Bb2, gs, "TG1")
        prod(TG2, 16, Cn1, Cn2, gs, "TG2")
        prod(X1, 8, Bb1, Bb2, gs, "X1")
        prod(X1b, 8, E1, E2, gs, "X1b")
        prod(OUTM[:, :, 384:512], 24, Cn1, Cn2, gs, "outm")
        prod(OUTM[:, :, 512:640], 24, F1, F2, gs, "outm")
        for g in range(8):
            gg = oc * 8 + g
            b = ps_bank(k, "tp")
            pb = k.PS[b][:, 0:128].bitcast(BF16).rearrange("p (a t) -> p a t", a=2)
            S.op("pe", lambda e: e.transpose(out=pb[:, 0, :], in_=X1[:, g, :], identity=k.ident[:]), reads=["X1", "ident"], writes=[("ps", b)])
            S.op("pe", lambda e: e.transpose(out=pb[:, 1, :], in_=X1b[:, g, :], identity=k.ident[:]), reads=["X1b", "ident"], writes=[("ps", b)])
            S.op("act", lambda e: e.activation(out=OUTM[:, g, 0:256].rearrange("p (a t) -> p a t", a=2), in_=pb, func=AF.Copy), reads=[("ps", b)], writes=["outm"])
            b2 = ps_bank(k, "pp")
            S.op("pe", lambda e: e.matmul(k.PS[b2][:, 0:128], lhsT=TG1[:, g, :], rhs=TG2[:, g, :], start=True, stop=True), reads=["TG1", "TG2"], writes=[("ps", b2)])
            tf = k.cnt["tf"] % 2
            k.cnt["tf"] += 1
            S.op("dve", lambda e: e.tensor_tensor(out=k.tmpf[tf][:, 0:128], in0=k.PS[b2][:, 0:128], in1=mask, op=ALU.mult), reads=[("ps", b2), "cf"], writes=[("tmpf", tf)])
            S.op("dve", lambda e: e.scalar_tensor_tensor(out=OUTM[:, g, 256:384], in0=identf, scalar=DG[:, gg:gg + 1], in1=k.tmpf[tf][:, 0:128], op0=ALU.mult, op1=ALU.add),
                 reads=[("tmpf", tf), "identf", "DG"], writes=["outm"])
        S.dma("sp", "t8", k.mats[:, gs, :], OUTM, reads=["outm"], writes=["mats"])


def hg_layer(k, lid, j, h_in, h_out):
    S = k.S
    I = k.I
    cf = k.cf
    layer_common_setup(k, lid)
    prep_weight(k, I["hg_w_in"][j], 4096, 0, gpre_t=k.gpre)
    prep_weight(k, I["hg_w_out"][j], 1024, 4)
    S.barrier()
    W = k.work
    o = [0]

    def alloc(n, dt=F32):
        v = W[:, o[0]:o[0] + n]
        o[0] += n
        return v if dt == F32 else v.bitcast(dt)
    ST = alloc(1024).rearrange("p (h v) -> p h v", h=8)
    SB = alloc(512, BF16).rearrange("p (h v) -> p h v", h=8)
    GN = alloc(1024)
    LB = alloc(8); OML = alloc(8)
    LG = alloc(32).rearrange("p (d h) -> p d h", d=4)
    LS = alloc(8); LM = alloc(8)
    FA = [alloc(512) for _ in range(2)]; FB = [alloc(512) for _ in range(2)]; FC = [alloc(512) for _ in range(2)]; FD = [alloc(512) for _ in range(2)]
    QT = [alloc(256, BF16) for _ in range(2)]; QA = [alloc(256, BF16) for _ in range(2)]; QB = [alloc(256, BF16) for _ in range(2)]
    KT = [alloc(256, BF16) for _ in range(2)]; KHT = [alloc(256, BF16) for _ in range(2)]
    KH = [alloc(256, BF16).rearrange("p (t k) -> p t k", t=4) for _ in range(2)]
    AT = [alloc(64, BF16) for _ in range(2)]
    TO = [alloc(128) for _ in range(2)]
    assert o[0] <= W.shape[1], o[0]
    RM = cf[:, 512:1024]
    BCM = cf[:, 1024:1152]
    EM8 = cf[:, 1152:1160]
    OM8 = cf[:, 1160:1168]
    S.dma("sp", "g2", LG, I["hg_lb_logits"].rearrange("d (h p) -> p d h", p=128), writes=["LG"], allow_slow_non_contiguous=True)
    S.dma("act", "g3", GN, I["hg_norm"][j].partition_broadcast(128), writes=["GN"])
    S.op("dve", lambda e: e.tensor_tensor(out=LM, in0=LG[:, 0, :], in1=LG[:, 1, :], op=ALU.max), reads=["LG"], writes=["LM"])
    S.op("dve", lambda e: e.tensor_tensor(out=LM, in0=LM, in1=LG[:, 2, :], op=ALU.max), reads=["LG", "LM"], writes=["LM"])
    S.op("dve", lambda e: e.tensor_tensor(out=LM, in0=LM, in1=LG[:, 3, :], op=ALU.max), reads=["LG", "LM"], writes=["LM"])
    S.op("dve", lambda e: e.tensor_tensor(out=LG, in0=LG, in1=_bc(LM.unsqueeze(1), [128, 4, 8]), op=ALU.subtract), reads=["LG", "LM"], writes=["LG"])
    S.op("act", lambda e: e.activation(out=LG, in_=LG, func=AF.Exp), reads=["LG"], writes=["LG"])
    S.op("dve", lambda e: e.tensor_tensor(out=LS, in0=LG[:, 0, :], in1=LG[:, 1, :], op=ALU.add), reads=["LG"], writes=["LS"])
    S.op("dve", lambda e: e.tensor_tensor(out=LS, in0=LS, in1=LG[:, 2, :], op=ALU.add), reads=["LG", "LS"], writes=["LS"])
    S.op("dve", lambda e: e.tensor_tensor(out=LS, in0=LS, in1=LG[:, 3, :], op=ALU.add), reads=["LG", "LS"], writes=["LS"])
    S.op("dve", lambda e: e.reciprocal(out=LS, in_=LS), reads=["LS"], writes=["LS"])
    S.op("dve", lambda e: e.memset(LB, 0.0), writes=["LB"])
    for d in range(1, lid + 1):
        S.op("dve", lambda e: e.tensor_tensor(out=LB, in0=LB, in1=LG[:, d, :], op=ALU.add), reads=["LG", "LB"], writes=["LB"])
    S.op("dve", lambda e: e.tensor_tensor(out=LB, in0=LB, in1=LS, op=ALU.mult), reads=["LS", "LB"], writes=["LB"])
    S.op("dve", lambda e: e.tensor_scalar(out=OML, in0=LB, scalar1=-1.0, scalar2=1.0, op0=ALU.mult, op1=ALU.add), reads=["LB"], writes=["OML"])
    S.op("dve", lambda e: e.memset(ST, 0.0), writes=[("st", h) for h in range(8)])
    S.op("pool", lambda e: e.memset(SB, 0.0), writes=[("sb", h) for h in range(8)])
    hv_in = hview(h_in, "jp")
    hv_out = hview(h_out, "jp")
    cnt = {"p": 0, "r": 0, "o": 0, "kv": 0}
    for blk in range(k.nblk):
        norm_and_transpose(k, hv_in, blk)
        wi = load_slot(k, 2)

        def evac_v(s, half, b):
            dst = k.TM[1][:, s, half * 512:(half + 1) * 512]
            if (s + half) % 2 == 0:
                S.op("act", lambda e: e.activation(out=dst, in_=k.PS[b][:, :], func=AF.Copy), reads=[("ps", b)], writes=[("tm1", s)])
            else:
                S.op("dve", lambda e: e.tensor_copy(out=dst, in_=k.PS[b][:, :]), reads=[("ps", b)], writes=[("tm1", s)])
        project(k, wi, 1024, evac_v)
        wi = load_slot(k, 3, eng="act")

        def evac_z(s, half, b):
            S.op("act", lambda e: e.activation(out=k.TM[2][:, s, half * 512:(half + 1) * 512], in_=k.PS[b][:, :], func=AF.Silu),
                 reads=[("ps", b)], writes=[("tm2", s)])
        project(k, wi, 1024, evac_z)
        wq = load_slot(k, 0)
        wf = load_slot(k, 1, eng="act")
        for half in range(2):
            for head in range(8):
                p = cnt["p"] % 2
                cnt["p"] += 1
                hc = slice(head * 128, (head + 1) * 128)
                bq = ps_bank(k, "pp")
                bf = ps_bank(k, "pp")
                for (bb, ws) in ((bq, wq), (bf, wf)):
                    for kc in range(8):
                        S.op("pe", lambda e: e.matmul(k.PS[bb][:, :], lhsT=k.WS[ws][:, kc, hc], rhs=k.XT[:, kc, half * 4:(half + 1) * 4, :].rearrange("p s t -> p (s t)"),
                                                      start=(kc == 0), stop=(kc == 7)),
                             reads=[("xt", kc), ("ws", ws)], writes=[("ps", bb)])
                A, B, C, Dd = FA[p], FB[p], FC[p], FD[p]
                kA, kB, kC, kD = ("fa", p), ("fb", p), ("fc", p), ("fd", p)
                S.op("act", lambda e: e.activation(out=A, in_=k.PS[bf][:, :], func=AF.Sigmoid), reads=[("ps", bf)], writes=[kA])
                S.op("dve", lambda e: e.tensor_scalar(out=A, in0=A, scalar1=OML[:, head:head + 1], scalar2=LB[:, head:head + 1], op0=ALU.mult, op1=ALU.add),
                     reads=[kA, "OML", "LB"], writes=[kA])
                S.op("act", lambda e: e.activation(out=B, in_=A, func=AF.Ln), reads=[kA], writes=[kB])
                S.op("dve", lambda e: e.tensor_tensor_scan(out=C, data0=RM, data1=B, initial=0.0, op0=ALU.mult, op1=ALU.add), reads=[kB, "cf"], writes=[kC])
                S.op("act", lambda e: e.activation(out=B, in_=C, func=AF.Exp), reads=[kC], writes=[kB])
                S.op("pool", lambda e: e.tensor_scalar(out=Dd, in0=A, scalar1=-1.0, scalar2=1.0, op0=ALU.mult, op1=ALU.add), reads=[kA], writes=[kD])
                S.op("act", lambda e: e.activation(out=A, in_=C, func=AF.Exp, scale=-1.0), reads=[kC, kD], writes=[kA])
                S.op("dve", lambda e: e.tensor_tensor(out=QT[p], in0=k.PS[bq][:, :], in1=B, op=ALU.mult), reads=[("ps", bq), kB], writes=[("qt", p)])
                S.op("dve", lambda e: e.tensor_tensor(out=Dd, in0=Dd, in1=A, op=ALU.mult), reads=[kA, kD], writes=[kD])
                q3 = QT[p].rearrange("p (c t) -> p c t", c=8)
                S.op("pool", lambda e: e.tensor_tensor(out=QA[p].rearrange("p (c t) -> p c t", c=8), in0=q3, in1=_bc(EM8.unsqueeze(2), [128, 8, 64]), op=ALU.mult),
                     reads=[("qt", p), "cf"], writes=[("qa", p)])
                S.op("pool", lambda e: e.tensor_tensor(out=QB[p].rearrange("p (c t) -> p c t", c=8), in0=q3, in1=_bc(OM8.unsqueeze(2), [128, 8, 64]), op=ALU.mult),
                     reads=[("qt", p), "cf"], writes=[("qb", p)])
                S.op("act", lambda e: e.activation(out=KT[p], in_=Dd, func=AF.Copy), reads=[kD], writes=[("kt", p)])
                B3 = B.rearrange("p (c t) -> p c t", c=8)
                S.op("dve", lambda e: e.tensor_tensor(out=KHT[p].rearrange("p (c t) -> p c t", c=8), in0=Dd.rearrange("p (c t) -> p c t", c=8),
                                                      in1=_bc(B3[:, :, 63:64], [128, 8, 64]), op=ALU.mult), reads=[kD, kB], writes=[("kht", p)])
                b = ps_bank(k, "tp")
                pb = k.PS[b][:, 0:256].bitcast(BF16).rearrange("p (t x) -> p t x", t=4)
                for tile in range(4):
                    S.op("pe", lambda e: e.transpose(out=pb[:, tile, :], in_=KHT[p][:, tile * 128:(tile + 1) * 128], identity=k.ident[:]),
                         reads=[("kht", p), "ident"], writes=[("ps", b)])
                S.op("act", lambda e: e.activation(out=KH[p], in_=pb, func=AF.Copy), reads=[("ps", b)], writes=[("kh", p)])
                for tile in range(4):
                    s = half * 4 + tile
                    cols = slice(tile * 128, (tile + 1) * 128)
                    r = cnt["r"] % 2
                    cnt["r"] += 1
                    sc = k.PS[6][:, 0:128]
                    S.op("pe", lambda e: e.matmul(sc, lhsT=KT[p][:, cols], rhs=QT[p][:, cols], start=True, stop=True), reads=[("kt", p), ("qt", p)], writes=[("ps", 6)])
                    S.op("dve", lambda e: e.tensor_tensor(out=AT[r], in0=sc, in1=BCM, op=ALU.mult), reads=[("ps", 6), "cf"], writes=[("at", r)])
                    ob = ps_bank(k, "pp")
                    cnt["o"] += 1
                    oreg = k.PS[ob][:, 0:128]
                    V = k.TM[1][:, s, hc]
                    S.op("pe", lambda e: e.matmul(oreg, lhsT=AT[r], rhs=V, start=True, stop=False), reads=[("at", r), ("tm1", s)], writes=[("ps", ob)])
                    S.op("pe", lambda e: e.matmul(oreg, lhsT=QA[p][:, cols], rhs=SB[:, head, :], start=False, stop=False), reads=[("qa", p), ("sb", head)], writes=[("ps", ob)])
                    for ch in range(2):
                        kvr = k.PS[7][:, 0:128]
                        rows = slice(ch * 64, (ch + 1) * 64)
                        S.op("pe", lambda e: e.matmul(kvr, lhsT=KH[p][rows, tile, :], rhs=k.TM[1][rows, s, hc], start=True, stop=True),
                             reads=[("kh", p), ("tm1", s)], writes=[("ps", 7)])
                        cidx = 2 * tile + ch
                        S.op("dve", lambda e: e.scalar_tensor_tensor(out=ST[:, head, :], in0=ST[:, head, :], scalar=B[:, 64 * cidx + 63:64 * cidx + 64], in1=kvr,
                                                                    op0=ALU.mult, op1=ALU.add), reads=[("ps", 7), kB, ("st", head)], writes=[("st", head)])
                        S.op("act", lambda e: e.activation(out=SB[:, head, :], in_=ST[:, head, :], func=AF.Copy), reads=[("st", head)], writes=[("sb", head)])
                        if ch == 0:
                            S.op("pe", lambda e: e.matmul(oreg, lhsT=QB[p][:, cols], rhs=SB[:, head, :], start=False, stop=True), reads=[("qb", p), ("sb", head)], writes=[("ps", ob)])
                    S.op("act", lambda e: e.activation(out=k.junk[:, 0:128], in_=oreg, func=AF.Square, accum_out=k.small[:, 40:41]), reads=[("ps", ob)], writes=["junk", "hs0"])
                    S.op("act", lambda e: e.activation(out=k.small[:, 41:42], in_=k.small[:, 40:41], func=AF.Sqrt, scale=1.0 / 128, bias=EPS), reads=["hs0"], writes=["hs1"])
                    S.op("dve", lambda e: e.reciprocal(out=k.small[:, 42:43], in_=k.small[:, 41:42]), reads=["hs1"], writes=["hs2"])
                    to = TO[cnt["o"] % 2]
                    tk = ("to", cnt["o"] % 2)
                    S.op("dve", lambda e: e.scalar_tensor_tensor(out=to, in0=oreg, scalar=k.small[:, 42:43], in1=GN[:, hc], op0=ALU.mult, op1=ALU.mult),
                         reads=[("ps", ob), "hs2", "GN"], writes=[tk])
                    S.op("pool", lambda e: e.tensor_tensor(out=k.TM[2][:, s, hc], in0=k.TM[2][:, s, hc], in1=to, op=ALU.mult), reads=[tk, ("tm2", s)], writes=[("tm2", s)])
        transpose_tm(k, 2)
        wi = load_slot(k, 4)
        out_proj_residual(k, wi, hv_in, hv_out, blk)


def at_layer(k, lid, j, h_in, h_out):
    S = k.S
    I = k.I
    cf = k.cf
    layer_common_setup(k, lid)
    prep_weight(k, I["at_w_in"][j], 2560, 0, gpre_t=k.gpre)
    prep_weight(k, I["at_w_out"][j], 1024, 3)
    S.barrier()
    W = k.work
    o = [0]

    def alloc(n, dt=F32):
        v = W[:, o[0]:o[0] + n]
        o[0] += n
        return v if dt == F32 else v.bitcast(dt)
    CS = alloc(2048).rearrange("p (t f) -> p t f", t=64)
    SN = alloc(2048).rearrange("p (t f) -> p t f", t=64)
    QAUG = [alloc(1024, BF16).rearrange("p (h d) -> p h d", h=16)] * 2
    KAUG = [alloc(256, BF16).rearrange("p (h d) -> p h d", h=4) for _ in range(2)]
    VAUG = [alloc(256, BF16).rearrange("p (h d) -> p h d", h=4) for _ in range(2)]
    QT = alloc(1024, BF16).rearrange("p (h t) -> p h t", h=16)
    KT = [alloc(256, BF16).rearrange("p (h t) -> p h t", h=4) for _ in range(2)]
    RT = [alloc(256).rearrange("p (h f) -> p h f", h=8) for _ in range(4)]
    EC = [alloc(256, BF16)] * 2
    EP = [alloc(256, BF16)] * 2
    ESK = alloc(16)
    DEN = [alloc(4) for _ in range(2)]
    OT = [alloc(256)] * 2
    OF = alloc(512).rearrange("p (h d) -> p h d", h=4)
    QF = alloc(512)
    MB = alloc(512, BF16)
    assert o[0] <= W.shape[1], o[0]
    S.dma("sp", "a0", MB, I["c_bf"], writes=["MB"])
    browf = k.TM[0][0:1, :, :].rearrange("p a b -> p (a b)").bitcast(F32)[:, 0:1536]
    S.dma("sp", "a1", browf, I["at_b_in"][j].unsqueeze(0), writes=["browf"])
    S.op("dve", lambda e: e.tensor_copy(out=k.brow[:, 0:1536], in_=browf), reads=["browf"], writes=["brow"])
    S.dma("act", "a2", ESK, I["at_sinks"][j].partition_broadcast(128), writes=["ESK"])
    S.op("act", lambda e: e.activation(out=ESK, in_=ESK, func=AF.Exp), reads=["ESK"], writes=["ESK"])
    for i in range(2):
        S.op("pool", lambda e: e.memset(VAUG[i], 1.0), writes=[("vaug", i)])
        S.op("pool", lambda e: e.memset(KAUG[i], 0.0), writes=[("kaug", i)])
    S.op("pool", lambda e: e.memset(QAUG[0], 0.0), writes=[("qaug", 0), ("qaug", 1)])
    scr = k.TM[1][:, :, :].rearrange("p a b -> p (a b)").bitcast(F32)
    identf = scr[:, 0:128]
    POSI = scr[:, 128:192].bitcast(I32)
    POST = scr[:, 384:448]
    pv = I["positions"].rearrange("(t p) -> p t", p=128)
    for t8 in range(8):
        S.dma("sp" if t8 % 2 == 0 else "act", "a3", POSI[:, t8 * 8:(t8 + 1) * 8], pv[:, t8 * 8:(t8 + 1) * 8], writes=["POSI"], allow_slow_non_contiguous=True)
    S.op("dve", lambda e: e.tensor_copy(out=POST, in_=POSI), reads=["POSI"], writes=["POST"])
    xtf = k.XT[:, :, :, :].rearrange("p a b c -> p (a b c)").bitcast(F32)
    ANG = xtf[:, 0:2048].rearrange("p (t f) -> p t f", t=64)
    KK = xtf[:, 2048:4096].rearrange("p (t f) -> p t f", t=64)
    KI = scr[:, 2048:4096].bitcast(I32).rearrange("p (t f) -> p t f", t=64)
    invf = cf[:, 432:464]
    S.op("dve", lambda e: e.tensor_tensor(out=ANG, in0=_bc(POST.unsqueeze(2), [128, 64, 32]), in1=_bc(invf.unsqueeze(1), [128, 64, 32]), op=ALU.mult),
         reads=["POST", "cf"], writes=["ANG"])
    sin_of_g(k, ANG, 0.0, KK, KI, SN, ["ANG"], "ask", "rope")
    sin_of_g(k, ANG, math.pi / 2, KK, KI, CS, ["ANG"], "ask", "rope")
    S.barrier()
    dbg = 9.0
    if dbg <= 1:
        return
    hv_in = hview(h_in, "jp")
    hv_out = hview(h_out, "jp")
    cnt = {"rt": 0, "e": 0, "o": 0}

    def rope(src2, dst_aug, nh, gt, h0, rk, wk):
        S.op("act", lambda e: e.activation(out=QF[:, 0:nh * 64], in_=src2, func=AF.Copy), reads=rk, writes=["qf"])
        src4 = QF[:, 0:nh * 64].rearrange("p (h a f) -> p h a f", h=nh, a=2)
        cs = _bc(CS[:, gt, :].unsqueeze(1), [128, nh, 32])
        sn = _bc(SN[:, gt, :].unsqueeze(1), [128, nh, 32])
        t1 = src4[:, :, 0, :]
        t2 = src4[:, :, 1, :]
        r = [RT[i][:, 0:nh, :] for i in range(4)]
        S.op("dve", lambda e: e.tensor_tensor(out=r[0], in0=t1, in1=cs, op=ALU.mult), reads=["qf", "rope"], writes=[("rt", 0)])
        S.op("dve", lambda e: e.tensor_tensor(out=r[1], in0=t2, in1=sn, op=ALU.mult), reads=["qf", "rope"], writes=[("rt", 1)])
        S.op("pool", lambda e: e.tensor_tensor(out=r[2], in0=t2, in1=cs, op=ALU.mult), reads=["qf", "rope"], writes=[("rt", 2)])
        S.op("pool", lambda e: e.tensor_tensor(out=r[3], in0=t1, in1=sn, op=ALU.mult), reads=["qf", "rope"], writes=[("rt", 3)])
        S.op("dve", lambda e: e.tensor_tensor(out=dst_aug[:, h0:h0 + nh, 0:32], in0=r[0], in1=r[1], op=ALU.subtract), reads=[("rt", 0), ("rt", 1)], writes=wk)
        S.op("dve", lambda e: e.tensor_tensor(out=dst_aug[:, h0:h0 + nh, 32:64], in0=r[2], in1=r[3], op=ALU.add), reads=[("rt", 2), ("rt", 3)], writes=wk)

    for blk in range(k.nblk):
        norm_and_transpose(k, hv_in, blk)
        w2 = load_slot(k, 2, ncols=512)
        for s in range(8):
            b = ps_bank(k, "pp")
            for kc in range(8):
                S.op("pe", lambda e: e.matmul(k.PS[b][:, :], lhsT=k.XT[:, kc, s, :], rhs=k.WS[w2][:, kc, 0:512], start=(kc == 0), stop=(kc == 7)),
                     reads=[("xt", kc), ("ws", w2)], writes=[("ps", b)])
            S.op("act", lambda e: e.activation(out=k.TM[2][:, s, 512:1024], in_=k.PS[b][:, :], func=AF.Silu), reads=[("ps", b)], writes=[("tm2", s)])
        w0 = load_slot(k, 0)
        w1 = load_slot(k, 1, eng="act")
        for s in range(8):
            gt = blk * 8 + s
            par = gt % 2
            for half in range(2):
                b = ps_bank(k, "pp")
                for kc in range(8):
                    S.op("pe", lambda e: e.matmul(k.PS[b][:, :], lhsT=k.XT[:, kc, s, :], rhs=k.WS[w0][:, kc, half * 512:(half + 1) * 512], start=(kc == 0), stop=False),
                         reads=[("xt", kc), ("ws", w0)], writes=[("ps", b)])
                S.op("pe", lambda e: e.matmul(k.PS[b][:, :], lhsT=k.ones_row[:, :], rhs=k.brow[:, half * 512:(half + 1) * 512], start=False, stop=True),
                     reads=["ones_row", "brow"], writes=[("ps", b)])
                rope(k.PS[b][:, :], QAUG[par], 8, gt, half * 8, [("ps", b)], [("qaug", 0)])
            b = ps_bank(k, "pp")
            for kc in range(8):
                S.op("pe", lambda e: e.matmul(k.PS[b][:, :], lhsT=k.XT[:, kc, s, :], rhs=k.WS[w1][:, kc, 0:512], start=(kc == 0), stop=False),
                     reads=[("xt", kc), ("ws", w1)], writes=[("ps", b)])
            S.op("pe", lambda e: e.matmul(k.PS[b][:, :], lhsT=k.ones_row[:, :], rhs=k.brow[:, 1024:1536], start=False, stop=True),
                 reads=["ones_row", "brow"], writes=[("ps", b)])
            rope(k.PS[b][:, 0:256], KAUG[par], 4, gt, 0, [("ps", b)], [("kaug", par)])
            if dbg >= 2.4:
                S.op("act", lambda e: e.activation(out=VAUG[par][:, :, 0:64], in_=k.PS[b][:, 256:512].rearrange("p (h d) -> p h d", h=4), func=AF.Copy),
                     reads=[("ps", b)], writes=[("vaug", par)])
            b = ps_bank(k, "pp")
            for kc in range(8):
                S.op("pe", lambda e: e.matmul(k.PS[b][:, :], lhsT=k.XT[:, kc, s, :], rhs=k.WS[w1][:, kc, 512:1024], start=(kc == 0), stop=(kc == 7)),
                     reads=[("xt", kc), ("ws", w1)], writes=[("ps", b)])
            S.op("act", lambda e: e.activation(out=k.TM[2][:, s, 0:512], in_=k.PS[b][:, :], func=AF.Silu), reads=[("ps", b)], writes=[("tm2", s)])
            if dbg < 2.6:
                continue
            for hb in range(2):
                b = ps_bank(k, "tp")
                pb = k.PS[b][:, :].bitcast(BF16).rearrange("p (h t) -> p h t", h=8)
                for h in range(8):
                    S.op("pe", lambda e: e.transpose(out=pb[:, h, :], in_=QAUG[0][:, hb * 8 + h, :], identity=k.ident[:]),
                         reads=[("qaug", 0), "ident"], writes=[("ps", b)])
                if hb == 0:
                    S.op("act", lambda e: e.activation(out=QT[:, 0:8, :], in_=pb, func=AF.Copy), reads=[("ps", b)], writes=["qt"])
                else:
                    S.op("dve", lambda e: e.tensor_copy(out=QT[:, 8:16, :], in_=pb), reads=[("ps", b)], writes=["qt"])
            b = ps_bank(k, "tp")
            pb = k.PS[b][:, 0:256].bitcast(BF16).rearrange("p (h t) -> p h t", h=4)
            for h in range(4):
                S.op("pe", lambda e: e.transpose(out=pb[:, h, :], in_=KAUG[par][:, h, :], identity=k.ident[:]),
                     reads=[("kaug", par), "ident"], writes=[("ps", b)])
            S.op("act", lambda e: e.activation(out=KT[par][:, :, :], in_=pb, func=AF.Copy), reads=[("ps", b)], writes=[("ktt", par)])
            for jkv in range(4 if dbg > 2 else 0):
                ee = cnt["e"] % 2
                cnt["e"] += 1
                qv = QT[:, 4 * jkv:4 * jkv + 4, :].rearrange("p h t -> p (h t)")
                S.op("pe", lambda e: e.matmul(k.PS[6][:, :], lhsT=KT[par][:, jkv, :], rhs=qv, start=True, stop=False), reads=[("ktt", par), "qt"], writes=[("ps", 6)])
                S.op("pe", lambda e: e.matmul(k.PS[6][:, :], lhsT=k.ident[:], rhs=MB[:, 0:512], start=False, stop=True), reads=["ident", "MB"], writes=[("ps", 6)])
                S.op("act", lambda e: e.activation(out=EC[ee], in_=k.PS[6][:, :], func=AF.Exp, scale=0.125), reads=[("ps", 6)], writes=[("ec", 0)])
                if gt > 0:
                    S.op("pe", lambda e: e.matmul(k.PS[7][:, :], lhsT=KT[1 - par][:, jkv, :], rhs=qv, start=True, stop=False), reads=[("ktt", 1 - par), "qt"], writes=[("ps", 7)])
                    S.op("pe", lambda e: e.matmul(k.PS[7][:, :], lhsT=k.ident[:], rhs=MB[:, 512:1024], start=False, stop=True), reads=["ident", "MB"], writes=[("ps", 7)])
                    S.op("act", lambda e: e.activation(out=EP[ee], in_=k.PS[7][:, :], func=AF.Exp, scale=0.125), reads=[("ps", 7)], writes=[("ep", 0)])
                if dbg <= 3:
                    continue
                ob = ps_bank(k, "pp")
                for g in range(4):
                    oreg = k.PS[ob][:, g * 128:(g + 1) * 128]
                    if gt > 0:
                        S.op("pe", lambda e: e.matmul(oreg, lhsT=EP[ee][:, g * 128:(g + 1) * 128], rhs=VAUG[1 - par][:, jkv, :], start=True, stop=False),
                             reads=[("ep", 0), ("vaug", 1 - par)], writes=[("ps", ob)])
                    S.op("pe", lambda e: e.matmul(oreg, lhsT=EC[ee][:, g * 128:(g + 1) * 128], rhs=VAUG[par][:, jkv, :], start=(gt == 0), stop=True),
                         reads=[("ec", 0), ("vaug", par)], writes=[("ps", ob)])
                S.op("act", lambda e: e.activation(out=OF.rearrange("p h d -> p (h d)"), in_=k.PS[ob][:, :], func=AF.Copy), reads=[("ps", ob)], writes=["of"])
                S.op("dve", lambda e: e.tensor_tensor(out=DEN[ee].unsqueeze(2), in0=OF[:, :, 64:65], in1=ESK[:, 4 * jkv:4 * jkv + 4].unsqueeze(2), op=ALU.add),
                     reads=["of", "ESK"], writes=[("den", ee)])
                S.op("dve", lambda e: e.reciprocal(out=DEN[ee], in_=DEN[ee]), reads=[("den", ee)], writes=[("den", ee)])
                S.op("dve", lambda e: e.tensor_tensor(out=OT[ee].rearrange("p (h d) -> p h d", h=4), in0=OF[:, :, 0:64], in1=_bc(DEN[ee].unsqueeze(2), [128, 4, 64]), op=ALU.mult),
                     reads=["of", ("den", ee)], writes=[("ot", 0)])
                gv = k.TM[2][:, s, 256 * jkv:256 * jkv + 256]
                S.op("pool", lambda e: e.tensor_tensor(out=gv, in0=gv, in1=OT[ee], op=ALU.mult), reads=[("ot", 0), ("tm2", s)], writes=[("tm2", s)])
        transpose_tm(k, 2)
        w3 = load_slot(k, 3)
        out_proj_residual(k, w3, hv_in, hv_out, blk)


_CACHE = {}


def run_layers(inputs, layer_ids, nblk=NBLK, h_override=None):
    import time as _t
    t0 = _t.time()
    key = (tuple(layer_ids), nblk)
    if key not in _CACHE:
        _CACHE[key] = build(layer_ids, nblk)
    print("[kernel] build %.1fs" % (_t.time() - t0), flush=True)
    nc = _CACHE[key]
    consts = host_consts()
    consts_bf = host_consts_bf()
    ident = np.eye(128).astype(ml_dtypes.bfloat16)
    names = set(nc._in_names)
    in_maps = []
    for c in range(NCORES):
        b = c % 2
        m = {}
        for kname, v in inputs.items():
            if kname not in names:
                continue
            v = np.asarray(v)
            if kname == "x":
                m["x"] = np.ascontiguousarray((h_override if h_override is not None else v)[b])
            elif kname == "positions":
                m["positions"] = np.ascontiguousarray(v[b]).astype(np.int32)
            else:
                m[kname] = np.ascontiguousarray(v)
        m["c_ident"] = ident
        m["c_f32"] = consts
        m["c_bf"] = consts_bf
        in_maps.append(m)
    t0 = _t.time()
    res = run_bass_kernel_spmd(nc, in_maps, core_ids=list(range(NCORES)))
    print("[kernel] run %.1fs" % (_t.time() - t0), flush=True)
    out = np.stack([np.asarray(res.results[0]["out"]), np.asarray(res.results[1]["out"])], axis=0)
    return out


def kernel(**inputs):
    h = np.asarray(inputs["x"], dtype=np.float32)
    for lid in range(4):
        h = run_layers(inputs, [lid], h_override=h)
    return h.astype(np.float32)
```

```python
import math
import os
import numpy as np
import ml_dtypes
import concourse.bass as bass
import concourse.mybir as mybir
from concourse.bass_utils import run_bass_kernel_spmd

F32 = mybir.dt.float32
BF16 = mybir.dt.bfloat16
I32 = mybir.dt.int32
AF = mybir.ActivationFunctionType
ALU = mybir.AluOpType
AX = mybir.AxisListType

L = 8192
D = 1024
NBLK = 8
NCORES = 2
EPS = 1e-6
TWO_PI = 2.0 * math.pi


class Sched:
    def __init__(self, nc):
        self.nc = nc
        self.eng = {"pe": nc.tensor, "act": nc.scalar, "dve": nc.vector, "pool": nc.gpsimd, "sp": nc.sync}
        self.esem = {}
        self.ecnt = {}
        self._ctx = []
        self.sems = {}
        for e in self.eng:
            s = nc.semaphore("s_" + e)
            self.esem[e] = s.__enter__()
            self._ctx.append(s)
            self.ecnt[e] = 0
            self.sems["e:" + e] = self.esem[e]
        self.dsem = {}
        self.dcnt = {}
        self.lastw = {}
        self.readers = {}
        self.waited = {}
        self.nins = 0

    def _dslot(self, slot):
        if slot not in self.dsem:
            s = self.nc.semaphore("d_%d" % len(self.dsem))
            self.dsem[slot] = s.__enter__()
            self._ctx.append(s)
            self.dcnt[slot] = 0
            self.sems["d:" + str(slot)] = self.dsem[slot]
        return self.dsem[slot]

    def _wait(self, e, n, v):
        if self.waited.get((e, n), 0) < v:
            self.eng[e].wait_ge(self.sems[n], v)
            self.waited[(e, n)] = v

    def _deps(self, e, reads, writes):
        need = {}

        def add(t):
            if t is None:
                return
            n, v = t
            if need.get(n, 0) < v:
                need[n] = v
        for r in reads:
            add(self.lastw.get(r))
        for w in writes:
            add(self.lastw.get(w))
            for t in self.readers.get(w, ()):
                add(t)
        for n, v in need.items():
            self._wait(e, n, v)

    def _commit(self, tok, reads, writes):
        for r in reads:
            self.readers.setdefault(r, []).append(tok)
        for w in writes:
            self.lastw[w] = tok
            self.readers[w] = []

    def op(self, e, fn, reads=(), writes=()):
        self._deps(e, reads, writes)
        ins = fn(self.eng[e])
        self.ecnt[e] += 1
        ins.then_inc(self.esem[e], 1)
        self._commit(("e:" + e, self.ecnt[e]), reads, writes)
        self.nins += 1
        return ins

    def dma(self, e, slot, out, in_, reads=(), writes=(), **kw):
        sem = self._dslot(slot)
        self._deps(e, reads, writes)
        ins = self.eng[e].dma_start(out=out, in_=in_, **kw)
        self.dcnt[slot] += 16
        ins.then_inc(sem, 16)
        self._commit(("d:" + str(slot), self.dcnt[slot]), reads, writes)
        self.nins += 1
        return ins

    def barrier(self, rotate=False):
        for e in self.eng:
            for x in self.eng:
                if self.ecnt[x] > 0 and x != e:
                    self._wait(e, "e:" + x, self.ecnt[x])
            for sl, c in self.dcnt.items():
                if c > 0:
                    self._wait(e, "d:" + str(sl), c)
        if rotate:
            self.gen = getattr(self, "gen", 0) + 1
            old = dict(self.ecnt)
            olds = dict(self.esem)
            for e in self.eng:
                s = self.nc.semaphore("s_%s_%d" % (e, self.gen))
                self.esem[e] = s.__enter__()
                self._ctx.append(s)
                self.sems["e:" + e] = self.esem[e]
                self.ecnt[e] = 0
            self.lastw = {kk: vv for kk, vv in self.lastw.items() if vv[0].startswith("d:")}
            self.readers = {kk: [t for t in vv if t[0].startswith("d:")] for kk, vv in self.readers.items()}
            self.waited = {kk: vv for kk, vv in self.waited.items() if kk[1].startswith("d:")}

    def finish(self, keys, e="sp"):
        self._deps(e, list(keys), [])


class K:
    pass


def _bc(ap, shape):
    return ap.to_broadcast(list(shape))


def build(layer_ids, nblk=NBLK, out_from=None):
    nc = bass.Bass("TRN2", target_bir_lowering=False)
    k = K()
    k.nc = nc
    k.nblk = nblk
    S = Sched(nc)
    k.S = S
    din = lambda name, shape, dt=F32: nc.dram_tensor(name, list(shape), dt, kind="ExternalInput").ap()
    dint = lambda name, shape, dt=F32: nc.dram_tensor(name, list(shape), dt).ap()
    T = lambda name, shape, dt: nc.sbuf_tensor(name, list(shape), dt).__enter__()
    k.T = T
    I = {}
    din_pos = din if any(l % 3 == 2 for l in layer_ids) else (lambda *a, **kw: None)
    I["x"] = din("x", [L, D])
    I["positions"] = din_pos("positions", [L], I32)
    I["norm_pre"] = din("norm_pre", [4, D])
    I["norm_post"] = din("norm_post", [4, D])
    kinds = set(l % 3 for l in layer_ids)
    if 0 not in kinds:
        din_s5 = lambda *a, **kw: None
    else:
        din_s5 = din
    din_hg = din if 1 in kinds else (lambda *a, **kw: None)
    din_at = din if 2 in kinds else (lambda *a, **kw: None)
    I["s5_w_in"] = din_s5("s5_w_in", [2, D, 2048])
    I["s5_lambda_re"] = din_s5("s5_lambda_re", [2, 64, 64])
    I["s5_lambda_im"] = din_s5("s5_lambda_im", [2, 64, 64])
    I["s5_log_dt"] = din_s5("s5_log_dt", [2, 64])
    I["s5_b_re"] = din_s5("s5_b_re", [2, 64, 64, 16])
    I["s5_b_im"] = din_s5("s5_b_im", [2, 64, 64, 16])
    I["s5_c_re"] = din_s5("s5_c_re", [2, 64, 16, 64])
    I["s5_c_im"] = din_s5("s5_c_im", [2, 64, 16, 64])
    I["s5_d"] = din_s5("s5_d", [2, D])
    I["s5_w_glu"] = din_s5("s5_w_glu", [2, D, D])
    I["s5_b_glu"] = din_s5("s5_b_glu", [2, D])
    I["s5_w_out"] = din_s5("s5_w_out", [2, D, D])
    I["hg_w_in"] = din_hg("hg_w_in", [1, D, 4096])
    I["hg_lb_logits"] = din_hg("hg_lb_logits", [4, D])
    I["hg_norm"] = din_hg("hg_norm", [1, D])
    I["hg_w_out"] = din_hg("hg_w_out", [1, D, D])
    I["at_w_in"] = din_at("at_w_in", [1, D, 2560])
    I["at_b_in"] = din_at("at_b_in", [1, 1536])
    I["at_sinks"] = din_at("at_sinks", [1, 16])
    I["at_w_out"] = din_at("at_w_out", [1, D, D])
    I["c_ident"] = din("c_ident", [128, 128], BF16)
    I["c_f32"] = din("c_f32", [128, NCF])
    I["c_bf"] = din("c_bf", [128, 1024], BF16)
    k.I = I
    nc._in_names = [nm for nm, v in I.items() if v is not None]
    k.out = nc.dram_tensor("out", [L, D], F32, kind="ExternalOutput").ap()
    k.hbuf = [dint("hbuf0", [L, D]), dint("hbuf1", [L, D])]
    k.wbf = dint("wbf", [5, 128, 8, 1024], BF16)
    k.mats = dint("mats", [128, 64, 640], BF16)
    k.PS = [nc.psum_tensor("ps%d" % i, [128, 512], F32).__enter__() for i in range(8)]
    k.ident = T("ident", [128, 128], BF16)
    k.cf = T("cf", [128, NCF], F32)
    k.TM = [T("tm%d" % i, [128, 8, 1024], BF16) for i in range(3)]
    k.XT = T("xt", [128, 8, 8, 128], BF16)
    k.WS = [T("ws%d" % i, [128, 8, 1024], BF16) for i in range(2)]
    k.HS = [T("hs%d" % i, [128, 1024], F32) for i in range(3)]
    k.gpost = T("gpost", [128, 1024], F32)
    k.tmpf = [T("tmpf%d" % i, [128, 512], F32) for i in range(2)]
    k.tmpb = [T("tmpb%d" % i, [128, 512], BF16) for i in range(2)]
    k.junk = T("junk", [128, 1024], BF16)
    k.small = T("small", [128, 64], F32)
    k.ones_row = T("ones_row", [1, 128], BF16)
    k.brow = T("brow", [1, 1536], BF16)
    k.work = T("work", [128, 11136], F32)
    k.cnt = {"hs": 0, "ws": 0, "pp": 0, "tp": 0, "tf": 0, "tb": 0, "q": 0}

    S.dma("sp", "c0", k.ident[:], I["c_ident"], writes=["ident"])
    S.dma("sp", "c1", k.cf[:], I["c_f32"], writes=["cf"])
    S.op("pool", lambda e: e.memset(k.ones_row[:], 1.0), writes=["ones_row"])

    srcs = [I["x"]]
    n = len(layer_ids)
    for li in range(n):
        if li == n - 1:
            srcs.append(k.out)
        else:
            srcs.append(k.hbuf[li % 2])
    for li, lid in enumerate(layer_ids):
        kind, j = lid % 3, lid // 3
        if kind == 0:
            s5_layer(k, lid, j, srcs[li], srcs[li + 1])
        elif kind == 1:
            hg_layer(k, lid, j, srcs[li], srcs[li + 1])
        else:
            at_layer(k, lid, j, srcs[li], srcs[li + 1])
        S.barrier(rotate=True)
    S.finish([("hout", s) for s in range(8)] + ["hout"])
    S.barrier()
    return nc


def ps_bank(k, kind):
    if kind == "tp":
        b = k.cnt["tp"] % 2
        k.cnt["tp"] += 1
        return b
    if kind == "pp":
        b = 2 + k.cnt["pp"] % 4
        k.cnt["pp"] += 1
        return b
    raise ValueError


def prep_weight(k, w_ap, ncols, slot0, gpre_t=None):
    S = k.S
    wv = w_ap.rearrange("(kc p) n -> p kc n", p=128)
    PIECE = 256
    stage_f = [k.work[:, 0:2048].rearrange("p (a b) -> p a b", a=8), k.work[:, 2048:4096].rearrange("p (a b) -> p a b", a=8)]
    stage_b = [k.work[:, 4096:5120].bitcast(BF16).rearrange("p (a b) -> p a b", a=8),
               k.work[:, 5120:6144].bitcast(BF16).rearrange("p (a b) -> p a b", a=8)]
    for pi in range(ncols // PIECE):
        c0 = pi * PIECE
        q = k.cnt["q"] % 2
        k.cnt["q"] += 1
        S.dma("sp" if pi % 2 == 0 else "act", ("wst", q), stage_f[q], wv[:, :, c0:c0 + PIECE], writes=[("wstf", q)])
        if gpre_t is not None:
            S.op("dve", lambda e: e.tensor_tensor(out=stage_b[q], in0=stage_f[q], in1=_bc(gpre_t[:, :].unsqueeze(2), [128, 8, PIECE]), op=ALU.mult),
                 reads=[("wstf", q), "gpre"], writes=[("wstb", q)])
        else:
            S.op("act", lambda e: e.activation(out=stage_b[q], in_=stage_f[q], func=AF.Copy), reads=[("wstf", q)], writes=[("wstb", q)])
        slot = slot0 + c0 // 1024
        cc = c0 % 1024
        S.dma("pool", ("wsto", q), k.wbf[slot, :, :, cc:cc + PIECE], stage_b[q], reads=[("wstb", q)], writes=[("wbf", slot)])


def load_slot(k, slot, ncols=1024, eng="sp"):
    S = k.S
    i = k.cnt["ws"] % 2
    k.cnt["ws"] += 1
    S.dma(eng, ("ws", i), k.WS[i][:, :, 0:ncols], k.wbf[slot, :, :, 0:ncols], reads=[("wbf", slot)], writes=[("ws", i)])
    return i


def hview(h_ap, mode):
    if mode == "ci":
        return h_ap.rearrange("(b p s) d -> b s p d", p=128, s=8)
    return h_ap.rearrange("(b s p) d -> b s p d", p=128, s=8)


def norm_and_transpose(k, hv, blk):
    S = k.S
    for s in range(8):
        i = k.cnt["hs"] % 3
        k.cnt["hs"] += 1
        hs = k.HS[i]
        S.dma("sp" if s % 2 == 0 else "act", ("hs", i), hs[:], hv[blk, s], reads=["hsrc"], writes=[("hs", i)])
        S.op("act", lambda e: e.activation(out=k.junk[:], in_=hs[:], func=AF.Square, accum_out=k.small[:, s:s + 1]),
             reads=[("hs", i)], writes=["junk", ("ss", s)])
        S.op("act", lambda e: e.activation(out=k.small[:, 8 + s:9 + s], in_=k.small[:, s:s + 1], func=AF.Sqrt, scale=1.0 / D, bias=EPS),
             reads=[("ss", s)], writes=[("sq", s)])
        S.op("dve", lambda e: e.reciprocal(out=k.small[:, 16 + s:17 + s], in_=k.small[:, 8 + s:9 + s]), reads=[("sq", s)], writes=[("rs", s)])
        eng = "dve" if s % 2 == 0 else "pool"
        S.op(eng, lambda e: e.tensor_scalar(out=k.TM[0][:, s, :], in0=hs[:], scalar1=k.small[:, 16 + s:17 + s], scalar2=None, op0=ALU.mult),
             reads=[("hs", i), ("rs", s)], writes=[("tm0", s)])
    transpose_tm(k, 0)


def transpose_tm(k, ti):
    S = k.S
    for kc in range(8):
        b = ps_bank(k, "tp")
        pb = k.PS[b][:, :].bitcast(BF16).rearrange("p (s t) -> p s t", s=8)
        for s in range(8):
            S.op("pe", lambda e: e.transpose(out=pb[:, s, :], in_=k.TM[ti][:, s, kc * 128:(kc + 1) * 128], identity=k.ident[:]),
                 reads=[("tm%d" % ti, s), "ident"], writes=[("ps", b)])
        if kc % 2 == 0:
            S.op("act", lambda e: e.activation(out=k.XT[:, kc, :, :], in_=pb, func=AF.Copy), reads=[("ps", b)], writes=[("xt", kc)])
        else:
            S.op("dve", lambda e: e.tensor_copy(out=k.XT[:, kc, :, :], in_=pb), reads=[("ps", b)], writes=[("xt", kc)])


def project(k, wslot_i, ncols, evac, bias_cols=None):
    S = k.S
    W = k.WS[wslot_i]
    for s in range(8):
        for half in range(ncols // 512):
            b = ps_bank(k, "pp")
            for kc in range(8):
                S.op("pe", lambda e: e.matmul(k.PS[b][:, :], lhsT=k.XT[:, kc, s, :], rhs=W[:, kc, half * 512:(half + 1) * 512],
                                              start=(kc == 0), stop=(kc == 7 and (bias_cols is None or bias_cols(half) is None))),
                     reads=[("xt", kc), ("ws", wslot_i)], writes=[("ps", b)])
            if bias_cols is not None and bias_cols(half) is not None:
                o = bias_cols(half)
                S.op("pe", lambda e: e.matmul(k.PS[b][:, :], lhsT=k.ones_row[:, :], rhs=k.brow[:, o:o + 512], start=False, stop=True),
                     reads=["ones_row", "brow"], writes=[("ps", b)])
            evac(s, half, b)


def out_proj_residual(k, wslot_i, hv_in, hv_out, blk):
    S = k.S
    W = k.WS[wslot_i]
    for s in range(8):
        i = k.cnt["hs"] % 3
        k.cnt["hs"] += 1
        hs = k.HS[i]
        S.dma("sp" if s % 2 == 0 else "act", ("hs", i), hs[:], hv_in[blk, s], reads=["hsrc"], writes=[("hs", i)])
        banks = []
        for half in range(2):
            b = ps_bank(k, "pp")
            banks.append(b)
            for kc in range(8):
                S.op("pe", lambda e: e.matmul(k.PS[b][:, :], lhsT=k.XT[:, kc, s, :], rhs=W[:, kc, half * 512:(half + 1) * 512],
                                              start=(kc == 0), stop=(kc == 7)),
                     reads=[("xt", kc), ("ws", wslot_i)], writes=[("ps", b)])
            S.op("act", lambda e: e.activation(out=k.junk[:, 0:512], in_=k.PS[b][:, :], func=AF.Square, accum_out=k.small[:, 24 + half:25 + half]),
                 reads=[("ps", b)], writes=["junk", ("os", half)])
        S.op("dve", lambda e: e.tensor_tensor(out=k.small[:, 26:27], in0=k.small[:, 24:25], in1=k.small[:, 25:26], op=ALU.add),
             reads=[("os", 0), ("os", 1)], writes=["os2"])
        S.op("act", lambda e: e.activation(out=k.small[:, 27:28], in_=k.small[:, 26:27], func=AF.Sqrt, scale=1.0 / D, bias=EPS), reads=["os2"], writes=["os3"])
        S.op("dve", lambda e: e.reciprocal(out=k.small[:, 28:29], in_=k.small[:, 27:28]), reads=["os3"], writes=["os4"])
        for half in range(2):
            b = banks[half]
            tf = k.cnt["tf"] % 2
            k.cnt["tf"] += 1
            S.op("dve", lambda e: e.scalar_tensor_tensor(out=k.tmpf[tf][:, :], in0=k.PS[b][:, :], scalar=k.small[:, 28:29], in1=k.gpost[:, half * 512:(half + 1) * 512],
                                                        op0=ALU.mult, op1=ALU.mult),
                 reads=[("ps", b), "os4", "gpost"], writes=[("tmpf", tf)])
            S.op("pool", lambda e: e.tensor_tensor(out=hs[:, half * 512:(half + 1) * 512], in0=hs[:, half * 512:(half + 1) * 512], in1=k.tmpf[tf][:, :], op=ALU.add),
                 reads=[("tmpf", tf), ("hs", i)], writes=[("hs", i)])
        S.dma("pool", ("hso", i), hv_out[blk, s], hs[:], reads=[("hs", i)], writes=["hout", ("hout", s)])


def layer_common_setup(k, lid):
    S = k.S
    k.gpre = k.small[:, 32:40]
    S.dma("sp", "g0", k.gpre, k.I["norm_pre"][lid].rearrange("(kc p) -> p kc", p=128), writes=["gpre"], allow_slow_non_contiguous=True)
    S.dma("act", "g1", k.gpost[:], k.I["norm_post"][lid].partition_broadcast(128), writes=["gpost"])


NCF = 1184


def host_consts_bf():
    kk = np.arange(128)[:, None]
    qq = np.arange(128)[None, :]
    cur = np.where(kk <= qq, 0.0, -30000.0).astype(np.float32)
    prv = np.where(kk > qq, 0.0, -30000.0).astype(np.float32)
    c = np.concatenate([np.tile(cur, (1, 4)), np.tile(prv, (1, 4))], axis=1)
    return c.astype(ml_dtypes.bfloat16)


def host_consts():
    c = np.zeros((128, NCF), np.float32)
    c[:, 0:129] = 8.0 * np.arange(129, dtype=np.float32)[None, :]
    jv = np.concatenate([-np.arange(8), 7 - np.arange(8), np.arange(8), 1 + np.arange(8)]).astype(np.float32)
    c[:, 136:168] = jv[None, :]
    c[:64, 170] = -1.0
    c[64:, 170] = 1.0
    c[:64, 171] = 1.0
    c[64:, 171] = -1.0
    ii = np.arange(128) // 16
    c[:, 176:304] = (ii[:, None] <= ii[None, :]).astype(np.float32)
    for j in range(64):
        c[64 + j, 304 + j] = -1.0
        c[j, 304 + 64 + j] = 1.0
    c[:, 432:464] = (10000.0 ** (-np.arange(0, 64, 2, dtype=np.float32) / 64.0)).astype(np.float32)[None, :]
    c[:, 470] = np.arange(128, dtype=np.float32)
    rm = np.ones(512, np.float32)
    rm[::64] = 0.0
    c[:, 512:1024] = rm[None, :]
    s_ = np.arange(128)
    c[:, 1024:1152] = ((s_[:, None] // 64 == s_[None, :] // 64) & (s_[:, None] <= s_[None, :])).astype(np.float32)
    c[:, 1152:1160] = (np.arange(8) % 2 == 0).astype(np.float32)[None, :]
    c[:, 1160:1168] = (np.arange(8) % 2 == 1).astype(np.float32)[None, :]
    return c


def sin_of_g(k, ang, shift, kk, ki, dst, rk, skey, dkey):
    S = k.S
    C1 = 6.28125
    C2 = TWO_PI - 6.28125
    kf = ki.bitcast(F32)

    def dv(fn, reads, writes):
        S.op("dve", fn, reads=reads, writes=writes)
    dv(lambda e: e.tensor_scalar(out=kk, in0=ang, scalar1=shift, scalar2=1.0 / TWO_PI, op0=ALU.add, op1=ALU.mult), rk, [skey])
    dv(lambda e: e.tensor_copy(out=ki, in_=kk), [skey], [skey])
    dv(lambda e: e.tensor_copy(out=kk, in_=ki), [skey], [skey])
    dv(lambda e: e.scalar_tensor_tensor(out=kf, in0=kk, scalar=-C1, in1=ang, op0=ALU.mult, op1=ALU.add), rk + [skey], [skey])
    dv(lambda e: e.scalar_tensor_tensor(out=kf, in0=kk, scalar=-C2, in1=kf, op0=ALU.mult, op1=ALU.add), [skey], [skey])
    dv(lambda e: e.tensor_scalar(out=kf, in0=kf, scalar1=shift, scalar2=math.pi, op0=ALU.add, op1=ALU.min), [skey], [skey])
    dv(lambda e: e.tensor_scalar(out=kf, in0=kf, scalar1=-math.pi, scalar2=None, op0=ALU.max), [skey], [skey])
    S.op("act", lambda e: e.activation(out=dst, in_=kf, func=AF.Sin), reads=[skey], writes=[dkey])


def s5_layer(k, lid, j, h_in, h_out):
    S = k.S
    I = k.I
    cf = k.cf
    layer_common_setup(k, lid)
    prep_weight(k, I["s5_w_in"][j], 2048, 0, gpre_t=k.gpre)
    prep_weight(k, I["s5_w_glu"][j], 1024, 2)
    prep_weight(k, I["s5_w_out"][j], 1024, 3)
    S.barrier()
    browf = k.work[0:1, 0:1024]
    S.dma("sp", "b0", browf, I["s5_b_glu"][j].unsqueeze(0), writes=["browf"])
    S.op("dve", lambda e: e.tensor_copy(out=k.brow[:, 0:1024], in_=browf), reads=["browf"], writes=["brow"])
    S.barrier()
    s5_tables(k, j)
    S.barrier()
    hv_in = hview(h_in, "ci")
    hv_out = hview(h_out, "ci")
    W = k.work
    def carve(off, n, dt=F32):
        v = W[:, off:off + n]
        return v if dt == F32 else v.bitcast(dt)
    MQ = [carve(0, 1280, BF16).rearrange("p (g m) -> p g m", g=4), carve(1280, 1280, BF16).rearrange("p (g m) -> p g m", g=4)]
    UGq = [carve(2560, 256, BF16).rearrange("p (g m) -> p g m", g=4), carve(2816, 256, BF16).rearrange("p (g m) -> p g m", g=4)]
    tA = [carve(3072, 512).rearrange("p (g m) -> p g m", g=4), carve(3584, 512).rearrange("p (g m) -> p g m", g=4)]
    tB = [carve(4096, 512).rearrange("p (g m) -> p g m", g=4), carve(4608, 512).rearrange("p (g m) -> p g m", g=4)]
    ZT = [carve(5120, 512).rearrange("p (g m) -> p g m", g=4), carve(5632, 512).rearrange("p (g m) -> p g m", g=4)]
    ST = [carve(6144, 516).rearrange("p (g m) -> p g m", g=4), carve(6660, 516).rearrange("p (g m) -> p g m", g=4)]
    P1 = [carve(7176, 256, BF16).rearrange("p (g m) -> p g m", g=4), carve(7432, 256, BF16).rearrange("p (g m) -> p g m", g=4)]
    P2 = [carve(7688, 256, BF16).rearrange("p (g m) -> p g m", g=4), carve(7944, 256, BF16).rearrange("p (g m) -> p g m", g=4)]
    LAST = carve(8200, 64)
    CARRY = carve(8264, 64)
    CT = carve(8328, 64)
    COS = k.s5_COS
    SIN = k.s5_SIN
    RHO = k.s5_RHO
    C128 = k.s5_C128
    S128 = k.s5_S128
    Jm = cf[:, 304:432]
    S.op("dve", lambda e: e.memset(CARRY, 0.0), writes=["carry"])
    UT = k.TM[1][:, :, :].rearrange("p s (g h) -> p g s h", h=16)
    UTp = k.TM[1][:, :, :].rearrange("p a b -> p (a b)").rearrange("p (g i h) -> p g i h", g=64, i=8)
    for blk in range(k.nblk):
        norm_and_transpose(k, hv_in, blk)
        wi = load_slot(k, 0)

        def evac_u(s, half, b):
            src = k.PS[b][:, :].rearrange("p (g h) -> p g h", h=16)
            dst = UTp[:, half * 32:(half + 1) * 32, s, :]
            if (s + half) % 2 == 0:
                S.op("act", lambda e: e.activation(out=dst, in_=src, func=AF.Copy), reads=[("ps", b)], writes=[("tm1", "u")])
            else:
                S.op("dve", lambda e: e.tensor_copy(out=dst, in_=src), reads=[("ps", b)], writes=[("tm1", "u")])
        project(k, wi, 1024, evac_u)
        wi = load_slot(k, 1, eng="act")

        def evac_z(s, half, b):
            S.op("act", lambda e: e.activation(out=k.TM[2][:, s, half * 512:(half + 1) * 512], in_=k.PS[b][:, :], func=AF.Silu),
                 reads=[("ps", b)], writes=[("tm2", s)])
        project(k, wi, 1024, evac_z)
        for q in range(16):
            g0 = 4 * q
            qb = q % 2
            S.dma("sp" if q % 2 == 0 else "act", ("mq", qb), MQ[qb], k.mats[:, g0:g0 + 4, :], reads=["mats"], writes=[("mq", qb)])
            b = ps_bank(k, "tp")
            pb = k.PS[b][:, 0:256].bitcast(BF16).rearrange("p (g t) -> p g t", g=4)
            for jj in range(4):
                S.op("pe", lambda e: e.transpose(out=pb[:, jj, :], in_=UTp[:, g0 + jj, :, :].rearrange("p i h -> p (i h)"), identity=k.ident[:]),
                     reads=[("tm1", "u"), "ident"], writes=[("ps", b)])
            S.op("act", lambda e: e.activation(out=UGq[qb], in_=pb, func=AF.Copy), reads=[("ps", b)], writes=[("ugq", qb)])
            for jj in range(4):
                S.op("pe", lambda e: e.matmul(k.PS[6][:, jj * 128:(jj + 1) * 128], lhsT=MQ[qb][:, jj, 0:128], rhs=UGq[qb][:, jj, :], start=True, stop=True),
                     reads=[("mq", qb), ("ugq", qb)], writes=[("ps", 6)])
                S.op("pe", lambda e: e.matmul(k.PS[7][:, jj * 128:(jj + 1) * 128], lhsT=MQ[qb][:, jj, 128:256], rhs=UGq[qb][:, jj, :], start=True, stop=True),
                     reads=[("mq", qb), ("ugq", qb)], writes=[("ps", 7)])
            pz1 = k.PS[6][:, :].rearrange("p (g t) -> p g t", g=4)
            pz2 = k.PS[7][:, :].rearrange("p (g t) -> p g t", g=4)
            S.op("dve", lambda e: e.tensor_tensor(out=tA[qb], in0=pz1, in1=COS[:, g0:g0 + 4, 1:129], op=ALU.mult), reads=[("ps", 6), "s5tab"], writes=[("tA", qb)])
            S.op("dve", lambda e: e.tensor_tensor(out=tB[qb], in0=pz2, in1=SIN[:, g0:g0 + 4, 1:129], op=ALU.mult), reads=[("ps", 7), "s5tab"], writes=[("tB", qb)])
            S.op("pool", lambda e: e.tensor_tensor(out=ZT[qb], in0=tA[qb], in1=tB[qb], op=ALU.add), reads=[("tA", qb), ("tB", qb)], writes=[("ZT", qb)])
            S.op("pool", lambda e: e.tensor_copy(out=ST[qb][:, :, 0:1], in_=CARRY[:, g0:g0 + 4].unsqueeze(2)), reads=["carry"], writes=[("ST", qb)])
            for jj in range(4):
                S.op("dve", lambda e: e.tensor_tensor_scan(out=ST[qb][:, jj, 1:129], data0=_bc(RHO[:, g0 + jj:g0 + jj + 1], [128, 128]), data1=ZT[qb][:, jj, :],
                                                           initial=ST[qb][:, jj, 0:1], op0=ALU.mult, op1=ALU.add),
                     reads=[("ZT", qb), ("ST", qb), "s5tab"], writes=[("ST", qb)])
            S.op("pool", lambda e: e.tensor_tensor(out=P1[qb], in0=ST[qb][:, :, 0:128], in1=COS[:, g0:g0 + 4, 0:128], op=ALU.mult), reads=[("ST", qb), "s5tab"], writes=[("P1", qb)])
            S.op("dve", lambda e: e.tensor_tensor(out=P2[qb], in0=ST[qb][:, :, 0:128], in1=SIN[:, g0:g0 + 4, 0:128], op=ALU.mult), reads=[("ST", qb), "s5tab"], writes=[("P2", qb)])
            S.op("act", lambda e: e.activation(out=LAST[:, g0:g0 + 4].unsqueeze(2), in_=ST[qb][:, :, 128:129], func=AF.Copy), reads=[("ST", qb)], writes=["last"])
            yb = ps_bank(k, "pp")
            for jj in range(4):
                o = k.PS[yb][:, jj * 128:(jj + 1) * 128]
                S.op("pe", lambda e: e.matmul(o, lhsT=UGq[qb][:, jj, :], rhs=MQ[qb][:, jj, 256:384], start=True, stop=False), reads=[("ugq", qb), ("mq", qb)], writes=[("ps", yb)])
                S.op("pe", lambda e: e.matmul(o, lhsT=P1[qb][:, jj, :], rhs=MQ[qb][:, jj, 384:512], start=False, stop=False), reads=[("P1", qb), ("mq", qb)], writes=[("ps", yb)])
                S.op("pe", lambda e: e.matmul(o, lhsT=P2[qb][:, jj, :], rhs=MQ[qb][:, jj, 512:640], start=False, stop=True), reads=[("P2", qb), ("mq", qb)], writes=[("ps", yb)])
            for jj in range(4):
                src = k.PS[yb][:, jj * 128:(jj + 1) * 128].rearrange("p (i h) -> p i h", h=16)
                dst = k.TM[0][:, :, 16 * (g0 + jj):16 * (g0 + jj) + 16]
                S.op("act", lambda e: e.activation(out=dst, in_=src, func=AF.Gelu_apprx_tanh), reads=[("ps", yb)], writes=[("tm0", s) for s in range(8)])
        S.op("pe", lambda e: e.matmul(k.PS[6][:, 0:64], lhsT=Jm, rhs=LAST, start=True, stop=True), reads=["last", "cf"], writes=[("ps", 6)])
        S.op("dve", lambda e: e.tensor_tensor(out=CT, in0=k.PS[6][:, 0:64], in1=S128, op=ALU.mult), reads=[("ps", 6), "s5tab"], writes=["ct"])
        S.op("dve", lambda e: e.tensor_tensor(out=CARRY, in0=LAST, in1=C128, op=ALU.mult), reads=["last", "s5tab"], writes=["carry"])
        S.op("dve", lambda e: e.tensor_tensor(out=CARRY, in0=CARRY, in1=CT, op=ALU.add), reads=["ct", "carry"], writes=["carry"])
        transpose_tm(k, 0)
        wi = load_slot(k, 2)

        def evac_glu(s, half, b):
            tb = k.cnt["tb"] % 2
            k.cnt["tb"] += 1
            cs = slice(half * 512, (half + 1) * 512)
            S.op("act", lambda e: e.activation(out=k.tmpb[tb][:, :], in_=k.PS[b][:, :], func=AF.Sigmoid), reads=[("ps", b)], writes=[("tmpb", tb)])
            S.op("dve", lambda e: e.tensor_tensor(out=k.tmpb[tb][:, :], in0=k.tmpb[tb][:, :], in1=k.TM[0][:, s, cs], op=ALU.mult),
                 reads=[("tmpb", tb), ("tm0", s)], writes=[("tmpb", tb)])
            S.op("pool", lambda e: e.tensor_tensor(out=k.TM[2][:, s, cs], in0=k.TM[2][:, s, cs], in1=k.tmpb[tb][:, :], op=ALU.mult),
                 reads=[("tmpb", tb), ("tm2", s)], writes=[("tm2", s)])
        project(k, wi, 1024, evac_glu, bias_cols=lambda half: half * 512)
        transpose_tm(k, 2)
        wi = load_slot(k, 3, eng="act")
        out_proj_residual(k, wi, hv_in, hv_out, blk)


def s5_tables(k, j):
    S = k.S
    I = k.I
    cf = k.cf
    T = k.T
    if not hasattr(k, "s5_COS"):
        k.s5_COS = T("s5cos", [128, 64, 129], BF16)
        k.s5_SIN = T("s5sin", [128, 64, 129], BF16)
        k.s5_misc = T("s5misc", [128, 3, 64], F32)
    k.s5_RHO = k.s5_misc[:, 0, :]
    k.s5_C128 = k.s5_misc[:, 1, :]
    k.s5_S128 = k.s5_misc[:, 2, :]
    W = k.work
    o = [0]

    def alloc(n):
        v = W[:, o[0]:o[0] + n]
        o[0] += n
        return v

    def dv(fn, reads, writes, eng="dve"):
        S.op(eng, fn, reads=reads, writes=writes)
    LR = alloc(64); LI = alloc(64); DT = alloc(64); LRDT = alloc(64); LIDT = alloc(64)
    AR = alloc(64); AI = alloc(64); MAG = alloc(64); KK = alloc(64); KI = alloc(64).bitcast(I32)
    DEN = alloc(64); T1 = alloc(64); T2 = alloc(64); Q1 = alloc(64); Q2 = alloc(64); AM1 = alloc(64); A128 = alloc(64)
    DG = alloc(64)
    base = o[0]
    lre = I["s5_lambda_re"][j].rearrange("g p -> p g")
    lim = I["s5_lambda_im"][j].rearrange("g p -> p g")
    for hf in range(2):
        S.dma("sp", "t0", LR[hf * 64:(hf + 1) * 64, :], lre, writes=["LR"], allow_slow_non_contiguous=True)
        S.dma("act", "t1", LI[hf * 64:(hf + 1) * 64, :], lim, writes=["LI"], allow_slow_non_contiguous=True)
    S.dma("pool", "t2", DT, I["s5_log_dt"][j].partition_broadcast(128), writes=["DT"])
    S.op("act", lambda e: e.activation(out=DT, in_=DT, func=AF.Exp), reads=["DT"], writes=["DT"])
    dv(lambda e: e.tensor_tensor(out=LRDT, in0=LR, in1=DT, op=ALU.mult), ["LR", "DT"], ["LRDT"])
    dv(lambda e: e.tensor_tensor(out=LIDT, in0=LI, in1=DT, op=ALU.mult), ["LI", "DT"], ["LIDT"])

    def sin_of(ang, shift, kk, ki, dst, rk, skey, dkey):
        sin_of_g(k, ang, shift, kk, ki, dst, rk, skey, dkey)

    SGN = cf[:, 170:171]
    NS = cf[:, 171:172]
    S.op("act", lambda e: e.activation(out=MAG, in_=LRDT, func=AF.Exp), reads=["LRDT"], writes=["MAG"])
    sin_of(LIDT, 0.0, KK, KI, AI, ["LIDT"], "sk0", "AI")
    dv(lambda e: e.tensor_tensor(out=AI, in0=AI, in1=MAG, op=ALU.mult), ["AI", "MAG"], ["AI"])
    sin_of(LIDT, math.pi / 2, KK, KI, AR, ["LIDT"], "sk0", "AR")
    dv(lambda e: e.tensor_tensor(out=AR, in0=AR, in1=MAG, op=ALU.mult), ["AR", "MAG"], ["AR"])
    dv(lambda e: e.tensor_tensor(out=DEN, in0=LR, in1=LR, op=ALU.mult), ["LR"], ["DEN"])
    dv(lambda e: e.tensor_tensor(out=T1, in0=LI, in1=LI, op=ALU.mult), ["LI"], ["T1"])
    dv(lambda e: e.tensor_tensor(out=DEN, in0=DEN, in1=T1, op=ALU.add), ["DEN", "T1"], ["DEN"])
    dv(lambda e: e.reciprocal(out=DEN, in_=DEN), ["DEN"], ["DEN"])
    dv(lambda e: e.tensor_scalar(out=AM1, in0=AR, scalar1=-1.0, scalar2=None, op0=ALU.add), ["AR"], ["AM1"])
    dv(lambda e: e.tensor_tensor(out=T1, in0=AM1, in1=LR, op=ALU.mult), ["AM1", "LR", "T1"], ["T1"])
    dv(lambda e: e.tensor_tensor(out=T2, in0=AI, in1=LI, op=ALU.mult), ["AI", "LI"], ["T2"])
    dv(lambda e: e.tensor_tensor(out=T1, in0=T1, in1=T2, op=ALU.add), ["T1", "T2"], ["T1"])
    dv(lambda e: e.tensor_tensor(out=Q1, in0=T1, in1=DEN, op=ALU.mult), ["T1", "DEN"], ["Q1"])
    dv(lambda e: e.tensor_tensor(out=T1, in0=AI, in1=LR, op=ALU.mult), ["AI", "LR", "T1"], ["T1"])
    dv(lambda e: e.tensor_tensor(out=T2, in0=AM1, in1=LI, op=ALU.mult), ["AM1", "LI", "T2"], ["T2"])
    dv(lambda e: e.tensor_tensor(out=T1, in0=T1, in1=T2, op=ALU.subtract), ["T1", "T2"], ["T1"])
    dv(lambda e: e.tensor_tensor(out=T1, in0=T1, in1=DEN, op=ALU.mult), ["T1", "DEN"], ["T1"])
    dv(lambda e: e.tensor_scalar(out=Q2, in0=T1, scalar1=SGN, scalar2=None, op0=ALU.mult), ["T1", "cf"], ["Q2"])
    S.op("act", lambda e: e.activation(out=k.s5_RHO, in_=LRDT, func=AF.Exp, scale=8.0), reads=["LRDT"], writes=["s5tab"])
    dv(lambda e: e.tensor_scalar(out=A128, in0=LIDT, scalar1=1024.0, scalar2=None, op0=ALU.mult), ["LIDT"], ["A128"])
    sin_of(A128, 0.0, KK, KI, k.s5_S128, ["A128"], "sk0", "s5tab")
    sin_of(A128, math.pi / 2, KK, KI, k.s5_C128, ["A128"], "sk0", "s5tab")
    dsrc = I["s5_d"][j].rearrange("(g h) -> h g", h=16)
    for i in range(8):
        S.dma("sp" if i % 2 == 0 else "act", "t7", DG[16 * i:16 * i + 16, :], dsrc, writes=["DG"], allow_slow_non_contiguous=True)
    ramp = cf[:, 0:129]
    o[0] = base
    ANG = alloc(8 * 129).rearrange("p (g c) -> p g c", g=8)
    KK2 = alloc(8 * 129).rearrange("p (g c) -> p g c", g=8)
    KI2 = alloc(8 * 129).bitcast(I32).rearrange("p (g c) -> p g c", g=8)
    for pc in range(8):
        gs = slice(pc * 8, pc * 8 + 8)
        dv(lambda e: e.tensor_tensor(out=ANG, in0=_bc(LIDT[:, gs].unsqueeze(2), [128, 8, 129]), in1=_bc(ramp.unsqueeze(1), [128, 8, 129]), op=ALU.mult),
           ["LIDT", "cf", "sk1"], ["ANG"])
        sin_of(ANG, 0.0, KK2, KI2, k.s5_SIN[:, gs, :], ["ANG"], "sk1", "s5tab")
        sin_of(ANG, math.pi / 2, KK2, KI2, k.s5_COS[:, gs, :], ["ANG"], "sk1", "s5tab")
    S.barrier()
    o[0] = base
    jv = cf[:, 136:168]
    PW1 = alloc(2048).rearrange("p (g j) -> p g j", g=64)
    PW2 = alloc(2048).rearrange("p (g j) -> p g j", g=64)
    pbase = o[0]
    KI3 = alloc(2048).bitcast(I32).rearrange("p (g j) -> p g j", g=64)
    MG3 = alloc(2048).rearrange("p (g j) -> p g j", g=64)
    xtf = k.XT[:, :, :, :].rearrange("p a b c -> p (a b c)").bitcast(F32)
    ANG3 = xtf[:, 0:2048].rearrange("p (g j) -> p g j", g=64)
    KK3 = xtf[:, 2048:4096].rearrange("p (g j) -> p g j", g=64)
    dv(lambda e: e.tensor_tensor(out=ANG3, in0=_bc(LIDT.unsqueeze(2), [128, 64, 32]), in1=_bc(jv.unsqueeze(1), [128, 64, 32]), op=ALU.mult), ["LIDT", "cf"], ["ANG3"])
    dv(lambda e: e.tensor_tensor(out=MG3, in0=_bc(LRDT.unsqueeze(2), [128, 64, 32]), in1=_bc(jv.unsqueeze(1), [128, 64, 32]), op=ALU.mult), ["LRDT", "cf"], ["MG3"])
    S.op("act", lambda e: e.activation(out=MG3, in_=MG3, func=AF.Exp), reads=["MG3"], writes=["MG3"])
    sin_of(ANG3, math.pi / 2, KK3, KI3, PW1, ["ANG3"], "sk2", "PW1")
    dv(lambda e: e.tensor_tensor(out=PW1, in0=PW1, in1=MG3, op=ALU.mult), ["PW1", "MG3"], ["PW1"])
    sin_of(ANG3, 0.0, KK3, KI3, PW2, ["ANG3"], "sk2", "PW2")
    dv(lambda e: e.tensor_tensor(out=PW2, in0=PW2, in1=MG3, op=ALU.mult), ["PW2", "MG3"], ["PW2"])
    dv(lambda e: e.tensor_scalar(out=PW2, in0=PW2, scalar1=SGN, scalar2=None, op0=ALU.mult), ["PW2", "cf"], ["PW2"])
    S.barrier()
    o[0] = pbase
    G3 = lambda v: v.rearrange("p (g h) -> p g h", g=64)
    Cs1 = G3(xtf[:, 0:1024]); Cs2 = G3(xtf[:, 1024:2048]); Bs1 = G3(xtf[:, 2048:3072]); Bs2 = G3(xtf[:, 3072:4096])
    CC1 = alloc(1024).rearrange("p (t c) -> p t c", t=8)
    CC2 = alloc(1024).rearrange("p (t c) -> p t c", t=8)
    Bb1 = G3(alloc(1024)); Bb2 = G3(alloc(1024)); TT = G3(alloc(1024))
    assert o[0] <= W.shape[1], o[0]
    tm0f = k.TM[0][:, :, :].rearrange("p a b -> p (a b)")[:, 4096:8192].bitcast(F32)
    E1 = G3(tm0f[:, 0:1024]); E2 = G3(tm0f[:, 1024:2048])
    tm2f = k.TM[2][:, :, :].rearrange("p a b -> p (a b)")[:, 5120:8192].bitcast(F32)
    Cn1 = G3(tm2f[:, 0:1024])
    scr = k.TM[1][:, :, :].rearrange("p a b -> p (a b)").bitcast(F32)
    PA = scr[:, 0:1024].rearrange("p (g i h) -> p g i h", g=8, i=8)
    PB = scr[:, 1024:2048].rearrange("p (g i h) -> p g i h", g=8, i=8)
    identf = scr[:, 2048:2176]
    Cn2 = G3(scr[:, 2176:3200])
    F1 = G3(CC1.rearrange("p t c -> p (t c)"))
    S.op("dve", lambda e: e.tensor_copy(out=identf, in_=k.ident[:]), reads=["ident"], writes=["identf"])
    bre = I["s5_b_re"][j].rearrange("g p h -> p g h")
    bim = I["s5_b_im"][j].rearrange("g p h -> p g h")
    S.dma("sp", "t3", Bs1[0:64], bre, writes=["Bs1"])
    S.dma("act", "t3", Bs1[64:128], bim, writes=["Bs1"])
    S.dma("sp", "t4", Bs2[0:64], bim, writes=["Bs2"])
    S.dma("act", "t4", Bs2[64:128], bre, writes=["Bs2"])
    cre = I["s5_c_re"][j].rearrange("(t g) h p -> (g h) t p", t=8)
    cim = I["s5_c_im"][j].rearrange("(t g) h p -> (g h) t p", t=8)
    S.dma("sp", "t5", CC1[:, :, 0:64], cre, writes=["CC1"])
    S.dma("act", "t5", CC1[:, :, 64:128], cim, writes=["CC1"])
    S.dma("sp", "t6", CC2[:, :, 0:64], cim, writes=["CC2"])
    S.dma("act", "t6", CC2[:, :, 64:128], cre, writes=["CC2"])
    for (CC, Cs, ck, sk) in ((CC1, Cs1, "CC1", "Cs1"), (CC2, Cs2, "CC2", "Cs2")):
        for t in range(8):
            b = ps_bank(k, "pp")
            S.op("pe", lambda e: e.transpose(out=k.PS[b][:, 0:128], in_=CC[:, t, :], identity=identf), reads=[ck, "identf"], writes=[("ps", b)])
            S.op("act", lambda e: e.activation(out=Cs[:, 8 * t:8 * t + 8, :], in_=k.PS[b][:, 0:128].rearrange("p (g h) -> p g h", g=8), func=AF.Copy),
                 reads=[("ps", b)], writes=[sk])
    Q1b = _bc(Q1.unsqueeze(2), [128, 64, 16]); Q2b = _bc(Q2.unsqueeze(2), [128, 64, 16])
    dv(lambda e: e.tensor_tensor(out=Bb1, in0=Bs1, in1=Q1b, op=ALU.mult), ["Bs1", "Q1"], ["Bb1"])
    dv(lambda e: e.tensor_tensor(out=TT, in0=Bs2, in1=Q2b, op=ALU.mult), ["Bs2", "Q2"], ["TT"])
    dv(lambda e: e.tensor_tensor(out=Bb1, in0=Bb1, in1=TT, op=ALU.add), ["Bb1", "TT"], ["Bb1"])
    dv(lambda e: e.tensor_tensor(out=Bb2, in0=Bs2, in1=Q1b, op=ALU.mult), ["Bs2", "Q1"], ["Bb2"])
    dv(lambda e: e.tensor_tensor(out=TT, in0=Bs1, in1=Q2b, op=ALU.mult), ["Bs1", "Q2", "TT"], ["TT"])
    dv(lambda e: e.tensor_tensor(out=Bb2, in0=Bb2, in1=TT, op=ALU.subtract), ["Bb2", "TT"], ["Bb2"])
    dv(lambda e: e.tensor_scalar(out=E1, in0=Bb2, scalar1=NS, scalar2=None, op0=ALU.mult), ["Bb2", "cf"], ["E1"])
    dv(lambda e: e.tensor_scalar(out=E2, in0=Bb1, scalar1=NS, scalar2=-1.0, op0=ALU.mult, op1=ALU.mult), ["Bb1", "cf"], ["E2"])
    dv(lambda e: e.tensor_scalar(out=Cn1, in0=Cs1, scalar1=NS, scalar2=None, op0=ALU.mult), ["Cs1", "cf"], ["Cn1"])
    dv(lambda e: e.tensor_scalar(out=Cn2, in0=Cs2, scalar1=NS, scalar2=None, op0=ALU.mult), ["Cs2", "cf"], ["Cn2"])
    dv(lambda e: e.tensor_scalar(out=F1, in0=Cs2, scalar1=-1.0, scalar2=None, op0=ALU.mult), ["Cs2", "CC1"], ["F1", "CC1"])
    F2 = Cs1
    mask = cf[:, 176:304]
    bsc = k.TM[0][:, :, :].rearrange("p a b -> p (a b)")
    OUTM = k.TM[2][:, :, :].rearrange("p a b -> p (a b)")[:, 0:8 * 640].rearrange("p (g m) -> p g m", g=8)
    TG1 = bsc[:, 0:1024].rearrange("p (g m) -> p g m", g=8)
    TG2 = bsc[:, 1024:2048].rearrange("p (g m) -> p g m", g=8)
    X1 = bsc[:, 2048:3072].rearrange("p (g m) -> p g m", g=8)
    X1b = bsc[:, 3072:4096].rearrange("p (g m) -> p g m", g=8)
    allk = ["PW1", "PW2", "Bb1", "Bb2", "E1", "E2", "Cn1", "Cn2", "F1", "Cs1"]

    def prod(dst, pj0, A1, A2, gs, tag):
        d4 = dst.rearrange("p g (i h) -> p g i h", i=8)
        pw1 = _bc(PW1[:, gs, pj0:pj0 + 8].unsqueeze(3), [128, 8, 8, 16])
        pw2 = _bc(PW2[:, gs, pj0:pj0 + 8].unsqueeze(3), [128, 8, 8, 16])
        a1 = _bc(A1[:, gs, :].unsqueeze(2), [128, 8, 8, 16])
        a2 = _bc(A2[:, gs, :].unsqueeze(2), [128, 8, 8, 16])
        dv(lambda e: e.tensor_tensor(out=PA, in0=pw1, in1=a1, op=ALU.mult), allk, ["PA"])
        dv(lambda e: e.tensor_tensor(out=PB, in0=pw2, in1=a2, op=ALU.mult), allk, ["PB"], eng="pool")
        dv(lambda e: e.tensor_tensor(out=d4, in0=PA, in1=PB, op=ALU.add), ["PA", "PB"], [tag])
    for oc in range(8):
        gs = slice(oc * 8, oc * 8 + 8)
        prod(TG1, 0, Bb1, Bb2, gs, "TG1")
        prod(TG2, 16, Cn1, Cn2, gs, "TG2")
        prod(X1, 8, Bb1, Bb2, gs, "X1")
        prod(X1b, 8, E1, E2, gs, "X1b")
        prod(OUTM[:, :, 384:512], 24, Cn1, Cn2, gs, "outm")
        prod(OUTM[:, :, 512:640], 24, F1, F2, gs, "outm")
        for g in range(8):
            gg = oc * 8 + g
            b = ps_bank(k, "tp")
            pb = k.PS[b][:, 0:128].bitcast(BF16).rearrange("p (a t) -> p a t", a=2)
            S.op("pe", lambda e: e.transpose(out=pb[:, 0, :], in_=X1[:, g, :], identity=k.ident[:]), reads=["X1", "ident"], writes=[("ps", b)])
            S.op("pe", lambda e: e.transpose(out=pb[:, 1, :], in_=X1b[:, g, :], identity=k.ident[:]), reads=["X1b", "ident"], writes=[("ps", b)])
            S.op("act", lambda e: e.activation(out=OUTM[:, g, 0:256].rearrange("p (a t) -> p a t", a=2), in_=pb, func=AF.Copy), reads=[("ps", b)], writes=["outm"])
            b2 = ps_bank(k, "pp")
            S.op("pe", lambda e: e.matmul(k.PS[b2][:, 0:128], lhsT=TG1[:, g, :], rhs=TG2[:, g, :], start=True, stop=True), reads=["TG1", "TG2"], writes=[("ps", b2)])
            tf = k.cnt["tf"] % 2
            k.cnt["tf"] += 1
            S.op("dve", lambda e: e.tensor_tensor(out=k.tmpf[tf][:, 0:128], in0=k.PS[b2][:, 0:128], in1=mask, op=ALU.mult), reads=[("ps", b2), "cf"], writes=[("tmpf", tf)])
            S.op("dve", lambda e: e.scalar_tensor_tensor(out=OUTM[:, g, 256:384], in0=identf, scalar=DG[:, gg:gg + 1], in1=k.tmpf[tf][:, 0:128], op0=ALU.mult, op1=ALU.add),
                 reads=[("tmpf", tf), "identf", "DG"], writes=["outm"])
        S.dma("sp", "t8", k.mats[:, gs, :], OUTM, reads=["outm"], writes=["mats"])


def hg_layer(k, lid, j, h_in, h_out):
    S = k.S
    I = k.I
    cf = k.cf
    layer_common_setup(k, lid)
    prep_weight(k, I["hg_w_in"][j], 4096, 0, gpre_t=k.gpre)
    prep_weight(k, I["hg_w_out"][j], 1024, 4)
    S.barrier()
    W = k.work
    o = [0]

    def alloc(n, dt=F32):
        v = W[:, o[0]:o[0] + n]
        o[0] += n
        return v if dt == F32 else v.bitcast(dt)
    ST = alloc(1024).rearrange("p (h v) -> p h v", h=8)
    SB = alloc(512, BF16).rearrange("p (h v) -> p h v", h=8)
    GN = alloc(1024)
    LB = alloc(8); OML = alloc(8)
    LG = alloc(32).rearrange("p (d h) -> p d h", d=4)
    LS = alloc(8); LM = alloc(8)
    FA = [alloc(512) for _ in range(2)]; FB = [alloc(512) for _ in range(2)]; FC = [alloc(512) for _ in range(2)]; FD = [alloc(512) for _ in range(2)]
    QT = [alloc(256, BF16) for _ in range(2)]; QA = [alloc(256, BF16) for _ in range(2)]; QB = [alloc(256, BF16) for _ in range(2)]
    KT = [alloc(256, BF16) for _ in range(2)]; KHT = [alloc(256, BF16) for _ in range(2)]
    KH = [alloc(256, BF16).rearrange("p (t k) -> p t k", t=4) for _ in range(2)]
    AT = [alloc(64, BF16) for _ in range(2)]
    TO = [alloc(128) for _ in range(2)]
    assert o[0] <= W.shape[1], o[0]
    RM = cf[:, 512:1024]
    BCM = cf[:, 1024:1152]
    EM8 = cf[:, 1152:1160]
    OM8 = cf[:, 1160:1168]
    S.dma("sp", "g2", LG, I["hg_lb_logits"].rearrange("d (h p) -> p d h", p=128), writes=["LG"], allow_slow_non_contiguous=True)
    S.dma("act", "g3", GN, I["hg_norm"][j].partition_broadcast(128), writes=["GN"])
    S.op("dve", lambda e: e.tensor_tensor(out=LM, in0=LG[:, 0, :], in1=LG[:, 1, :], op=ALU.max), reads=["LG"], writes=["LM"])
    S.op("dve", lambda e: e.tensor_tensor(out=LM, in0=LM, in1=LG[:, 2, :], op=ALU.max), reads=["LG", "LM"], writes=["LM"])
    S.op("dve", lambda e: e.tensor_tensor(out=LM, in0=LM, in1=LG[:, 3, :], op=ALU.max), reads=["LG", "LM"], writes=["LM"])
    S.op("dve", lambda e: e.tensor_tensor(out=LG, in0=LG, in1=_bc(LM.unsqueeze(1), [128, 4, 8]), op=ALU.subtract), reads=["LG", "LM"], writes=["LG"])
    S.op("act", lambda e: e.activation(out=LG, in_=LG, func=AF.Exp), reads=["LG"], writes=["LG"])
    S.op("dve", lambda e: e.tensor_tensor(out=LS, in0=LG[:, 0, :], in1=LG[:, 1, :], op=ALU.add), reads=["LG"], writes=["LS"])
    S.op("dve", lambda e: e.tensor_tensor(out=LS, in0=LS, in1=LG[:, 2, :], op=ALU.add), reads=["LG", "LS"], writes=["LS"])
    S.op("dve", lambda e: e.tensor_tensor(out=LS, in0=LS, in1=LG[:, 3, :], op=ALU.add), reads=["LG", "LS"], writes=["LS"])
    S.op("dve", lambda e: e.reciprocal(out=LS, in_=LS), reads=["LS"], writes=["LS"])
    S.op("dve", lambda e: e.memset(LB, 0.0), writes=["LB"])
    for d in range(1, lid + 1):
        S.op("dve", lambda e: e.tensor_tensor(out=LB, in0=LB, in1=LG[:, d, :], op=ALU.add), reads=["LG", "LB"], writes=["LB"])
    S.op("dve", lambda e: e.tensor_tensor(out=LB, in0=LB, in1=LS, op=ALU.mult), reads=["LS", "LB"], writes=["LB"])
    S.op("dve", lambda e: e.tensor_scalar(out=OML, in0=LB, scalar1=-1.0, scalar2=1.0, op0=ALU.mult, op1=ALU.add), reads=["LB"], writes=["OML"])
    S.op("dve", lambda e: e.memset(ST, 0.0), writes=[("st", h) for h in range(8)])
    S.op("pool", lambda e: e.memset(SB, 0.0), writes=[("sb", h) for h in range(8)])
    hv_in = hview(h_in, "jp")
    hv_out = hview(h_out, "jp")
    cnt = {"p": 0, "r": 0, "o": 0, "kv": 0}
    for blk in range(k.nblk):
        norm_and_transpose(k, hv_in, blk)
        wi = load_slot(k, 2)

        def evac_v(s, half, b):
            dst = k.TM[1][:, s, half * 512:(half + 1) * 512]
            if (s + half) % 2 == 0:
                S.op("act", lambda e: e.activation(out=dst, in_=k.PS[b][:, :], func=AF.Copy), reads=[("ps", b)], writes=[("tm1", s)])
            else:
                S.op("dve", lambda e: e.tensor_copy(out=dst, in_=k.PS[b][:, :]), reads=[("ps", b)], writes=[("tm1", s)])
        project(k, wi, 1024, evac_v)
        wi = load_slot(k, 3, eng="act")

        def evac_z(s, half, b):
            S.op("act", lambda e: e.activation(out=k.TM[2][:, s, half * 512:(half + 1) * 512], in_=k.PS[b][:, :], func=AF.Silu),
                 reads=[("ps", b)], writes=[("tm2", s)])
        project(k, wi, 1024, evac_z)
        wq = load_slot(k, 0)
        wf = load_slot(k, 1, eng="act")
        for half in range(2):
            for head in range(8):
                p = cnt["p"] % 2
                cnt["p"] += 1
                hc = slice(head * 128, (head + 1) * 128)
                bq = ps_bank(k, "pp")
                bf = ps_bank(k, "pp")
                for (bb, ws) in ((bq, wq), (bf, wf)):
                    for kc in range(8):
                        S.op("pe", lambda e: e.matmul(k.PS[bb][:, :], lhsT=k.WS[ws][:, kc, hc], rhs=k.XT[:, kc, half * 4:(half + 1) * 4, :].rearrange("p s t -> p (s t)"),
                                                      start=(kc == 0), stop=(kc == 7)),
                             reads=[("xt", kc), ("ws", ws)], writes=[("ps", bb)])
                A, B, C, Dd = FA[p], FB[p], FC[p], FD[p]
                kA, kB, kC, kD = ("fa", p), ("fb", p), ("fc", p), ("fd", p)
                S.op("act", lambda e: e.activation(out=A, in_=k.PS[bf][:, :], func=AF.Sigmoid), reads=[("ps", bf)], writes=[kA])
                S.op("dve", lambda e: e.tensor_scalar(out=A, in0=A, scalar1=OML[:, head:head + 1], scalar2=LB[:, head:head + 1], op0=ALU.mult, op1=ALU.add),
                     reads=[kA, "OML", "LB"], writes=[kA])
                S.op("act", lambda e: e.activation(out=B, in_=A, func=AF.Ln), reads=[kA], writes=[kB])
                S.op("dve", lambda e: e.tensor_tensor_scan(out=C, data0=RM, data1=B, initial=0.0, op0=ALU.mult, op1=ALU.add), reads=[kB, "cf"], writes=[kC])
                S.op("act", lambda e: e.activation(out=B, in_=C, func=AF.Exp), reads=[kC], writes=[kB])
                S.op("pool", lambda e: e.tensor_scalar(out=Dd, in0=A, scalar1=-1.0, scalar2=1.0, op0=ALU.mult, op1=ALU.add), reads=[kA], writes=[kD])
                S.op("act", lambda e: e.activation(out=A, in_=C, func=AF.Exp, scale=-1.0), reads=[kC, kD], writes=[kA])
                S.op("dve", lambda e: e.tensor_tensor(out=QT[p], in0=k.PS[bq][:, :], in1=B, op=ALU.mult), reads=[("ps", bq), kB], writes=[("qt", p)])
                S.op("dve", lambda e: e.tensor_tensor(out=Dd, in0=Dd, in1=A, op=ALU.mult), reads=[kA, kD], writes=[kD])
                q3 = QT[p].rearrange("p (c t) -> p c t", c=8)
                S.op("pool", lambda e: e.tensor_tensor(out=QA[p].rearrange("p (c t) -> p c t", c=8), in0=q3, in1=_bc(EM8.unsqueeze(2), [128, 8, 64]), op=ALU.mult),
                     reads=[("qt", p), "cf"], writes=[("qa", p)])
                S.op("pool", lambda e: e.tensor_tensor(out=QB[p].rearrange("p (c t) -> p c t", c=8), in0=q3, in1=_bc(OM8.unsqueeze(2), [128, 8, 64]), op=ALU.mult),
                     reads=[("qt", p), "cf"], writes=[("qb", p)])
                S.op("act", lambda e: e.activation(out=KT[p], in_=Dd, func=AF.Copy), reads=[kD], writes=[("kt", p)])
                B3 = B.rearrange("p (c t) -> p c t", c=8)
                S.op("dve", lambda e: e.tensor_tensor(out=KHT[p].rearrange("p (c t) -> p c t", c=8), in0=Dd.rearrange("p (c t) -> p c t", c=8),
                                                      in1=_bc(B3[:, :, 63:64], [128, 8, 64]), op=ALU.mult), reads=[kD, kB], writes=[("kht", p)])
                b = ps_bank(k, "tp")
                pb = k.PS[b][:, 0:256].bitcast(BF16).rearrange("p (t x) -> p t x", t=4)
                for tile in range(4):
                    S.op("pe", lambda e: e.transpose(out=pb[:, tile, :], in_=KHT[p][:, tile * 128:(tile + 1) * 128], identity=k.ident[:]),
                         reads=[("kht", p), "ident"], writes=[("ps", b)])
                S.op("act", lambda e: e.activation(out=KH[p], in_=pb, func=AF.Copy), reads=[("ps", b)], writes=[("kh", p)])
                for tile in range(4):
                    s = half * 4 + tile
                    cols = slice(tile * 128, (tile + 1) * 128)
                    r = cnt["r"] % 2
                    cnt["r"] += 1
                    sc = k.PS[6][:, 0:128]
                    S.op("pe", lambda e: e.matmul(sc, lhsT=KT[p][:, cols], rhs=QT[p][:, cols], start=True, stop=True), reads=[("kt", p), ("qt", p)], writes=[("ps", 6)])
                    S.op("dve", lambda e: e.tensor_tensor(out=AT[r], in0=sc, in1=BCM, op=ALU.mult), reads=[("ps", 6), "cf"], writes=[("at", r)])
                    ob = ps_bank(k, "pp")
                    cnt["o"] += 1
                    oreg = k.PS[ob][:, 0:128]
                    V = k.TM[1][:, s, hc]
                    S.op("pe", lambda e: e.matmul(oreg, lhsT=AT[r], rhs=V, start=True, stop=False), reads=[("at", r), ("tm1", s)], writes=[("ps", ob)])
                    S.op("pe", lambda e: e.matmul(oreg, lhsT=QA[p][:, cols], rhs=SB[:, head, :], start=False, stop=False), reads=[("qa", p), ("sb", head)], writes=[("ps", ob)])
                    for ch in range(2):
                        kvr = k.PS[7][:, 0:128]
                        rows = slice(ch * 64, (ch + 1) * 64)
                        S.op("pe", lambda e: e.matmul(kvr, lhsT=KH[p][rows, tile, :], rhs=k.TM[1][rows, s, hc], start=True, stop=True),
                             reads=[("kh", p), ("tm1", s)], writes=[("ps", 7)])
                        cidx = 2 * tile + ch
                        S.op("dve", lambda e: e.scalar_tensor_tensor(out=ST[:, head, :], in0=ST[:, head, :], scalar=B[:, 64 * cidx + 63:64 * cidx + 64], in1=kvr,
                                                                    op0=ALU.mult, op1=ALU.add), reads=[("ps", 7), kB, ("st", head)], writes=[("st", head)])
                        S.op("act", lambda e: e.activation(out=SB[:, head, :], in_=ST[:, head, :], func=AF.Copy), reads=[("st", head)], writes=[("sb", head)])
                        if ch == 0:
                            S.op("pe", lambda e: e.matmul(oreg, lhsT=QB[p][:, cols], rhs=SB[:, head, :], start=False, stop=True), reads=[("qb", p), ("sb", head)], writes=[("ps", ob)])
                    S.op("act", lambda e: e.activation(out=k.junk[:, 0:128], in_=oreg, func=AF.Square, accum_out=k.small[:, 40:41]), reads=[("ps", ob)], writes=["junk", "hs0"])
                    S.op("act", lambda e: e.activation(out=k.small[:, 41:42], in_=k.small[:, 40:41], func=AF.Sqrt, scale=1.0 / 128, bias=EPS), reads=["hs0"], writes=["hs1"])
                    S.op("dve", lambda e: e.reciprocal(out=k.small[:, 42:43], in_=k.small[:, 41:42]), reads=["hs1"], writes=["hs2"])
                    to = TO[cnt["o"] % 2]
                    tk = ("to", cnt["o"] % 2)
                    S.op("dve", lambda e: e.scalar_tensor_tensor(out=to, in0=oreg, scalar=k.small[:, 42:43], in1=GN[:, hc], op0=ALU.mult, op1=ALU.mult),
                         reads=[("ps", ob), "hs2", "GN"], writes=[tk])
                    S.op("pool", lambda e: e.tensor_tensor(out=k.TM[2][:, s, hc], in0=k.TM[2][:, s, hc], in1=to, op=ALU.mult), reads=[tk, ("tm2", s)], writes=[("tm2", s)])
        transpose_tm(k, 2)
        wi = load_slot(k, 4)
        out_proj_residual(k, wi, hv_in, hv_out, blk)


def at_layer(k, lid, j, h_in, h_out):
    S = k.S
    I = k.I
    cf = k.cf
    layer_common_setup(k, lid)
    prep_weight(k, I["at_w_in"][j], 2560, 0, gpre_t=k.gpre)
    prep_weight(k, I["at_w_out"][j], 1024, 3)
    S.barrier()
    W = k.work
    o = [0]

    def alloc(n, dt=F32):
        v = W[:, o[0]:o[0] + n]
        o[0] += n
        return v if dt == F32 else v.bitcast(dt)
    CS = alloc(2048).rearrange("p (t f) -> p t f", t=64)
    SN = alloc(2048).rearrange("p (t f) -> p t f", t=64)
    QAUG = [alloc(1024, BF16).rearrange("p (h d) -> p h d", h=16)] * 2
    KAUG = [alloc(256, BF16).rearrange("p (h d) -> p h d", h=4) for _ in range(2)]
    VAUG = [alloc(256, BF16).rearrange("p (h d) -> p h d", h=4) for _ in range(2)]
    QT = alloc(1024, BF16).rearrange("p (h t) -> p h t", h=16)
    KT = [alloc(256, BF16).rearrange("p (h t) -> p h t", h=4) for _ in range(2)]
    RT = [alloc(256).rearrange("p (h f) -> p h f", h=8) for _ in range(4)]
    EC = [alloc(256, BF16)] * 2
    EP = [alloc(256, BF16)] * 2
    ESK = alloc(16)
    DEN = [alloc(4) for _ in range(2)]
    OT = [alloc(256)] * 2
    OF = alloc(512).rearrange("p (h d) -> p h d", h=4)
    QF = alloc(512)
    MB = alloc(512, BF16)
    assert o[0] <= W.shape[1], o[0]
    S.dma("sp", "a0", MB, I["c_bf"], writes=["MB"])
    browf = k.TM[0][0:1, :, :].rearrange("p a b -> p (a b)").bitcast(F32)[:, 0:1536]
    S.dma("sp", "a1", browf, I["at_b_in"][j].unsqueeze(0), writes=["browf"])
    S.op("dve", lambda e: e.tensor_copy(out=k.brow[:, 0:1536], in_=browf), reads=["browf"], writes=["brow"])
    S.dma("act", "a2", ESK, I["at_sinks"][j].partition_broadcast(128), writes=["ESK"])
    S.op("act", lambda e: e.activation(out=ESK, in_=ESK, func=AF.Exp), reads=["ESK"], writes=["ESK"])
    for i in range(2):
        S.op("pool", lambda e: e.memset(VAUG[i], 1.0), writes=[("vaug", i)])
        S.op("pool", lambda e: e.memset(KAUG[i], 0.0), writes=[("kaug", i)])
    S.op("pool", lambda e: e.memset(QAUG[0], 0.0), writes=[("qaug", 0), ("qaug", 1)])
    scr = k.TM[1][:, :, :].rearrange("p a b -> p (a b)").bitcast(F32)
    identf = scr[:, 0:128]
    POSI = scr[:, 128:192].bitcast(I32)
    POST = scr[:, 384:448]
    pv = I["positions"].rearrange("(t p) -> p t", p=128)
    for t8 in range(8):
        S.dma("sp" if t8 % 2 == 0 else "act", "a3", POSI[:, t8 * 8:(t8 + 1) * 8], pv[:, t8 * 8:(t8 + 1) * 8], writes=["POSI"], allow_slow_non_contiguous=True)
    S.op("dve", lambda e: e.tensor_copy(out=POST, in_=POSI), reads=["POSI"], writes=["POST"])
    xtf = k.XT[:, :, :, :].rearrange("p a b c -> p (a b c)").bitcast(F32)
    ANG = xtf[:, 0:2048].rearrange("p (t f) -> p t f", t=64)
    KK = xtf[:, 2048:4096].rearrange("p (t f) -> p t f", t=64)
    KI = scr[:, 2048:4096].bitcast(I32).rearrange("p (t f) -> p t f", t=64)
    invf = cf[:, 432:464]
    S.op("dve", lambda e: e.tensor_tensor(out=ANG, in0=_bc(POST.unsqueeze(2), [128, 64, 32]), in1=_bc(invf.unsqueeze(1), [128, 64, 32]), op=ALU.mult),
         reads=["POST", "cf"], writes=["ANG"])
    sin_of_g(k, ANG, 0.0, KK, KI, SN, ["ANG"], "ask", "rope")
    sin_of_g(k, ANG, math.pi / 2, KK, KI, CS, ["ANG"], "ask", "rope")
    S.barrier()
    dbg = 9.0
    if dbg <= 1:
        return
    hv_in = hview(h_in, "jp")
    hv_out = hview(h_out, "jp")
    cnt = {"rt": 0, "e": 0, "o": 0}

    def rope(src2, dst_aug, nh, gt, h0, rk, wk):
        S.op("act", lambda e: e.activation(out=QF[:, 0:nh * 64], in_=src2, func=AF.Copy), reads=rk, writes=["qf"])
        src4 = QF[:, 0:nh * 64].rearrange("p (h a f) -> p h a f", h=nh, a=2)
        cs = _bc(CS[:, gt, :].unsqueeze(1), [128, nh, 32])
        sn = _bc(SN[:, gt, :].unsqueeze(1), [128, nh, 32])
        t1 = src4[:, :, 0, :]
        t2 = src4[:, :, 1, :]
        r = [RT[i][:, 0:nh, :] for i in range(4)]
        S.op("dve", lambda e: e.tensor_tensor(out=r[0], in0=t1, in1=cs, op=ALU.mult), reads=["qf", "rope"], writes=[("rt", 0)])
        S.op("dve", lambda e: e.tensor_tensor(out=r[1], in0=t2, in1=sn, op=ALU.mult), reads=["qf", "rope"], writes=[("rt", 1)])
        S.op("pool", lambda e: e.tensor_tensor(out=r[2], in0=t2, in1=cs, op=ALU.mult), reads=["qf", "rope"], writes=[("rt", 2)])
        S.op("pool", lambda e: e.tensor_tensor(out=r[3], in0=t1, in1=sn, op=ALU.mult), reads=["qf", "rope"], writes=[("rt", 3)])
        S.op("dve", lambda e: e.tensor_tensor(out=dst_aug[:, h0:h0 + nh, 0:32], in0=r[0], in1=r[1], op=ALU.subtract), reads=[("rt", 0), ("rt", 1)], writes=wk)
        S.op("dve", lambda e: e.tensor_tensor(out=dst_aug[:, h0:h0 + nh, 32:64], in0=r[2], in1=r[3], op=ALU.add), reads=[("rt", 2), ("rt", 3)], writes=wk)

    for blk in range(k.nblk):
        norm_and_transpose(k, hv_in, blk)
        w2 = load_slot(k, 2, ncols=512)
        for s in range(8):
            b = ps_bank(k, "pp")
            for kc in range(8):
                S.op("pe", lambda e: e.matmul(k.PS[b][:, :], lhsT=k.XT[:, kc, s, :], rhs=k.WS[w2][:, kc, 0:512], start=(kc == 0), stop=(kc == 7)),
                     reads=[("xt", kc), ("ws", w2)], writes=[("ps", b)])
            S.op("act", lambda e: e.activation(out=k.TM[2][:, s, 512:1024], in_=k.PS[b][:, :], func=AF.Silu), reads=[("ps", b)], writes=[("tm2", s)])
        w0 = load_slot(k, 0)
        w1 = load_slot(k, 1, eng="act")
        for s in range(8):
            gt = blk * 8 + s
            par = gt % 2
            for half in range(2):
                b = ps_bank(k, "pp")
                for kc in range(8):
                    S.op("pe", lambda e: e.matmul(k.PS[b][:, :], lhsT=k.XT[:, kc, s, :], rhs=k.WS[w0][:, kc, half * 512:(half + 1) * 512], start=(kc == 0), stop=False),
                         reads=[("xt", kc), ("ws", w0)], writes=[("ps", b)])
                S.op("pe", lambda e: e.matmul(k.PS[b][:, :], lhsT=k.ones_row[:, :], rhs=k.brow[:, half * 512:(half + 1) * 512], start=False, stop=True),
                     reads=["ones_row", "brow"], writes=[("ps", b)])
                rope(k.PS[b][:, :], QAUG[par], 8, gt, half * 8, [("ps", b)], [("qaug", 0)])
            b = ps_bank(k, "pp")
            for kc in range(8):
                S.op("pe", lambda e: e.matmul(k.PS[b][:, :], lhsT=k.XT[:, kc, s, :], rhs=k.WS[w1][:, kc, 0:512], start=(kc == 0), stop=False),
                     reads=[("xt", kc), ("ws", w1)], writes=[("ps", b)])
            S.op("pe", lambda e: e.matmul(k.PS[b][:, :], lhsT=k.ones_row[:, :], rhs=k.brow[:, 1024:1536], start=False, stop=True),
                 reads=["ones_row", "brow"], writes=[("ps", b)])
            rope(k.PS[b][:, 0:256], KAUG[par], 4, gt, 0, [("ps", b)], [("kaug", par)])
            if dbg >= 2.4:
                S.op("act", lambda e: e.activation(out=VAUG[par][:, :, 0:64], in_=k.PS[b][:, 256:512].rearrange("p (h d) -> p h d", h=4), func=AF.Copy),
                     reads=[("ps", b)], writes=[("vaug", par)])
            b = ps_bank(k, "pp")
            for kc in range(8):
                S.op("pe", lambda e: e.matmul(k.PS[b][:, :], lhsT=k.XT[:, kc, s, :], rhs=k.WS[w1][:, kc, 512:1024], start=(kc == 0), stop=(kc == 7)),
                     reads=[("xt", kc), ("ws", w1)], writes=[("ps", b)])
            S.op("act", lambda e: e.activation(out=k.TM[2][:, s, 0:512], in_=k.PS[b][:, :], func=AF.Silu), reads=[("ps", b)], writes=[("tm2", s)])
            if dbg < 2.6:
                continue
            for hb in range(2):
                b = ps_bank(k, "tp")
                pb = k.PS[b][:, :].bitcast(BF16).rearrange("p (h t) -> p h t", h=8)
                for h in range(8):
                    S.op("pe", lambda e: e.transpose(out=pb[:, h, :], in_=QAUG[0][:, hb * 8 + h, :], identity=k.ident[:]),
                         reads=[("qaug", 0), "ident"], writes=[("ps", b)])
                if hb == 0:
                    S.op("act", lambda e: e.activation(out=QT[:, 0:8, :], in_=pb, func=AF.Copy), reads=[("ps", b)], writes=["qt"])
                else:
                    S.op("dve", lambda e: e.tensor_copy(out=QT[:, 8:16, :], in_=pb), reads=[("ps", b)], writes=["qt"])
            b = ps_bank(k, "tp")
            pb = k.PS[b][:, 0:256].bitcast(BF16).rearrange("p (h t) -> p h t", h=4)
            for h in range(4):
                S.op("pe", lambda e: e.transpose(out=pb[:, h, :], in_=KAUG[par][:, h, :], identity=k.ident[:]),
                     reads=[("kaug", par), "ident"], writes=[("ps", b)])
            S.op("act", lambda e: e.activation(out=KT[par][:, :, :], in_=pb, func=AF.Copy), reads=[("ps", b)], writes=[("ktt", par)])
            for jkv in range(4 if dbg > 2 else 0):
                ee = cnt["e"] % 2
                cnt["e"] += 1
                qv = QT[:, 4 * jkv:4 * jkv + 4, :].rearrange("p h t -> p (h t)")
                S.op("pe", lambda e: e.matmul(k.PS[6][:, :], lhsT=KT[par][:, jkv, :], rhs=qv, start=True, stop=False), reads=[("ktt", par), "qt"], writes=[("ps", 6)])
                S.op("pe", lambda e: e.matmul(k.PS[6][:, :], lhsT=k.ident[:], rhs=MB[:, 0:512], start=False, stop=True), reads=["ident", "MB"], writes=[("ps", 6)])
                S.op("act", lambda e: e.activation(out=EC[ee], in_=k.PS[6][:, :], func=AF.Exp, scale=0.125), reads=[("ps", 6)], writes=[("ec", 0)])
                if gt > 0:
                    S.op("pe", lambda e: e.matmul(k.PS[7][:, :], lhsT=KT[1 - par][:, jkv, :], rhs=qv, start=True, stop=False), reads=[("ktt", 1 - par), "qt"], writes=[("ps", 7)])
                    S.op("pe", lambda e: e.matmul(k.PS[7][:, :], lhsT=k.ident[:], rhs=MB[:, 512:1024], start=False, stop=True), reads=["ident", "MB"], writes=[("ps", 7)])
                    S.op("act", lambda e: e.activation(out=EP[ee], in_=k.PS[7][:, :], func=AF.Exp, scale=0.125), reads=[("ps", 7)], writes=[("ep", 0)])
                if dbg <= 3:
                    continue
                ob = ps_bank(k, "pp")
                for g in range(4):
                    oreg = k.PS[ob][:, g * 128:(g + 1) * 128]
                    if gt > 0:
                        S.op("pe", lambda e: e.matmul(oreg, lhsT=EP[ee][:, g * 128:(g + 1) * 128], rhs=VAUG[1 - par][:, jkv, :], start=True, stop=False),
                             reads=[("ep", 0), ("vaug", 1 - par)], writes=[("ps", ob)])
                    S.op("pe", lambda e: e.matmul(oreg, lhsT=EC[ee][:, g * 128:(g + 1) * 128], rhs=VAUG[par][:, jkv, :], start=(gt == 0), stop=True),
                         reads=[("ec", 0), ("vaug", par)], writes=[("ps", ob)])
                S.op("act", lambda e: e.activation(out=OF.rearrange("p h d -> p (h d)"), in_=k.PS[ob][:, :], func=AF.Copy), reads=[("ps", ob)], writes=["of"])
                S.op("dve", lambda e: e.tensor_tensor(out=DEN[ee].unsqueeze(2), in0=OF[:, :, 64:65], in1=ESK[:, 4 * jkv:4 * jkv + 4].unsqueeze(2), op=ALU.add),
                     reads=["of", "ESK"], writes=[("den", ee)])
                S.op("dve", lambda e: e.reciprocal(out=DEN[ee], in_=DEN[ee]), reads=[("den", ee)], writes=[("den", ee)])
                S.op("dve", lambda e: e.tensor_tensor(out=OT[ee].rearrange("p (h d) -> p h d", h=4), in0=OF[:, :, 0:64], in1=_bc(DEN[ee].unsqueeze(2), [128, 4, 64]), op=ALU.mult),
                     reads=["of", ("den", ee)], writes=[("ot", 0)])
                gv = k.TM[2][:, s, 256 * jkv:256 * jkv + 256]
                S.op("pool", lambda e: e.tensor_tensor(out=gv, in0=gv, in1=OT[ee], op=ALU.mult), reads=[("ot", 0), ("tm2", s)], writes=[("tm2", s)])
        transpose_tm(k, 2)
        w3 = load_slot(k, 3)
        out_proj_residual(k, w3, hv_in, hv_out, blk)


_CACHE = {}


def run_layers(inputs, layer_ids, nblk=NBLK, h_override=None):
    import time as _t
    t0 = _t.time()
    key = (tuple(layer_ids), nblk)
    if key not in _CACHE:
        _CACHE[key] = build(layer_ids, nblk)
    print("[kernel] build %.1fs" % (_t.time() - t0), flush=True)
    nc = _CACHE[key]
    consts = host_consts()
    consts_bf = host_consts_bf()
    ident = np.eye(128).astype(ml_dtypes.bfloat16)
    names = set(nc._in_names)
    in_maps = []
    for c in range(NCORES):
        b = c % 2
        m = {}
        for kname, v in inputs.items():
            if kname not in names:
                continue
            v = np.asarray(v)
            if kname == "x":
                m["x"] = np.ascontiguousarray((h_override if h_override is not None else v)[b])
            elif kname == "positions":
                m["positions"] = np.ascontiguousarray(v[b]).astype(np.int32)
            else:
                m[kname] = np.ascontiguousarray(v)
        m["c_ident"] = ident
        m["c_f32"] = consts
        m["c_bf"] = consts_bf
        in_maps.append(m)
    t0 = _t.time()
    res = run_bass_kernel_spmd(nc, in_maps, core_ids=list(range(NCORES)))
    print("[kernel] run %.1fs" % (_t.time() - t0), flush=True)
    out = np.stack([np.asarray(res.results[0]["out"]), np.asarray(res.results[1]["out"])], axis=0)
    return out


def kernel(**inputs):
    out = run_layers(inputs, [0, 1, 2, 3])
    return out.astype(np.float32)
```
